# Optimizing a Trainium2 kernel written in Bass

```python
import functools
import jax, jax.numpy as jnp
from jax import lax
import numpy as np

D_MODEL = 1024
BATCH = 16
SEQ = 4096
DEPTH = 1
DEC_BATCH = 16
DEC_SEQ = 64
PAST_LEN = 1024

CHUNK = 64
HEAD_DIM = 64
A_Q_HEADS = 8
A_KV_HEADS = 2
A_GROUP = A_Q_HEADS // A_KV_HEADS
A_WINDOW = 128
A_BAND_CHUNKS = A_WINDOW // CHUNK + 1
A_REACH = (A_BAND_CHUNKS - 1) * CHUNK
B_HEADS = 4
B_PREV_CHUNKS = 8
B_BAND_CHUNKS = B_PREV_CHUNKS + 1
B_REACH = B_PREV_CHUNKS * CHUNK
REL_CLIP = 128
C_HEADS = 4
N_MEM = 256
FF_DIM = 2816
ROPE_THETA = 10000.0
EPS = 1e-6
NEG = -1e30
N_BRANCH = 3

A_Q = A_Q_HEADS * HEAD_DIM
A_KV = A_KV_HEADS * HEAD_DIM
B_W = B_HEADS * HEAD_DIM
C_W = C_HEADS * HEAD_DIM
IN_COLS = A_Q + 2 * A_KV + 3 * B_W + C_W
SPLITS = (A_Q, A_Q + A_KV, A_Q + 2 * A_KV, A_Q + 2 * A_KV + B_W,
          A_Q + 2 * A_KV + 2 * B_W, A_Q + 2 * A_KV + 3 * B_W)

kernel_name = 'hybrid_streaming_encoder_step'


def rmsnorm(x, g):
    xf = x.astype(jnp.float32)
    y = xf * lax.rsqrt(jnp.mean(xf * xf, axis=-1, keepdims=True) + EPS)
    return (y * g.astype(jnp.float32)).astype(x.dtype)


def swiglu(x, w_gate, w_up, w_down):
    return (jax.nn.silu(x @ w_gate) * (x @ w_up)) @ w_down


def rope(x, pos):
    half = HEAD_DIM // 2
    inv = ROPE_THETA ** (-jnp.arange(half, dtype=jnp.float32) / half)
    ang = pos.astype(jnp.float32)[:, None] * inv[None, :]
    cos = jnp.cos(ang)[None, :, None, :]
    sin = jnp.sin(ang)[None, :, None, :]
    xf = x.astype(jnp.float32)
    x1, x2 = xf[..., :half], xf[..., half:]
    return jnp.concatenate([x1 * cos - x2 * sin, x1 * sin + x2 * cos], axis=-1).astype(x.dtype)


def band_mask(q_pos, k_pos, band_chunks):
    qc = q_pos[:, None] // CHUNK
    kc = k_pos[None, :] // CHUNK
    return (k_pos[None, :] >= 0) & (kc <= qc) & (qc - kc < band_chunks)


def swa_sink_attend(q, k, v, q_pos, k_pos, sinks):
    b, sq = q.shape[:2]
    qg = q.reshape(b, sq, A_KV_HEADS, A_GROUP, HEAD_DIM)
    s = jnp.einsum('bqkgd,bskd->bkgqs', qg, k).astype(jnp.float32) * (HEAD_DIM ** -0.5)
    s = jnp.where(band_mask(q_pos, k_pos, A_BAND_CHUNKS), s, NEG)
    sink = sinks.astype(jnp.float32).reshape(1, A_KV_HEADS, A_GROUP, 1, 1)
    sink = jnp.broadcast_to(sink, s.shape[:-1] + (1,))
    p = jax.nn.softmax(jnp.concatenate([s, sink], axis=-1), axis=-1)[..., :-1]
    o = jnp.einsum('bkgqs,bskd->bqkgd', p.astype(v.dtype), v)
    return o.reshape(b, sq, A_Q)


def chunk_relpos_attend(q, k, v, q_pos, k_pos, rel_bias):
    b, sq = q.shape[:2]
    s = jnp.einsum('bqhd,bshd->bhqs', q, k).astype(jnp.float32) * (HEAD_DIM ** -0.5)
    rel = jnp.clip(q_pos[:, None] - k_pos[None, :], -REL_CLIP, REL_CLIP) + REL_CLIP
    s = s + rel_bias.astype(jnp.float32)[:, rel][None]
    s = jnp.where(band_mask(q_pos, k_pos, B_BAND_CHUNKS), s, NEG)
    p = jax.nn.softmax(s, axis=-1)
    o = jnp.einsum('bhqs,bshd->bqhd', p.astype(v.dtype), v)
    return o.reshape(b, sq, B_W)


def mem_attend(q, mk, mv):
    b, sq = q.shape[:2]
    s = jnp.einsum('bqhd,bmhd->bhqm', q, mk).astype(jnp.float32) * (HEAD_DIM ** -0.5)
    p = jax.nn.softmax(s, axis=-1)
    o = jnp.einsum('bhqm,bmhd->bqhd', p.astype(mv.dtype), mv)
    return o.reshape(b, sq, C_W)


def memory_kv(mem, g_mem, w_mem_kv, g_kc):
    m = rmsnorm(mem, g_mem) @ w_mem_kv
    b, n = m.shape[:2]
    mk = rmsnorm(m[..., :C_W].reshape(b, n, C_HEADS, HEAD_DIM), g_kc)
    mv = m[..., C_W:].reshape(b, n, C_HEADS, HEAD_DIM)
    return mk, mv


def sweep_chunks(core, reach, q, k, v):
    b, s = q.shape[:2]
    nc = s // CHUNK
    kp = jnp.pad(k, ((0, 0), (reach, 0), (0, 0), (0, 0)))
    vp = jnp.pad(v, ((0, 0), (reach, 0), (0, 0), (0, 0)))

    def one(c):
        start = c * CHUNK
        qc = lax.dynamic_slice_in_dim(q, start, CHUNK, axis=1)
        kc = lax.dynamic_slice_in_dim(kp, start, reach + CHUNK, axis=1)
        vc = lax.dynamic_slice_in_dim(vp, start, reach + CHUNK, axis=1)
        q_pos = start + jnp.arange(CHUNK, dtype=jnp.int32)
        k_pos = start - reach + jnp.arange(reach + CHUNK, dtype=jnp.int32)
        return core(qc, kc, vc, q_pos, k_pos)

    out = lax.map(one, jnp.arange(nc, dtype=jnp.int32))
    return out.transpose(1, 0, 2, 3).reshape(b, s, out.shape[-1])


def cached_attend(core, cache_k, cache_v, q, k, v):
    n_past = cache_k.shape[1]
    k_all = jnp.concatenate([cache_k, k], axis=1)
    v_all = jnp.concatenate([cache_v, v], axis=1)
    k_pos = jnp.arange(PAST_LEN - n_past, PAST_LEN + q.shape[1], dtype=jnp.int32)
    return core(q, k_all, v_all, k_pos[n_past:], k_pos)


def trunk_layer(x, pos, mem_k, mem_v, attn_a, attn_b, w):
    b, s, _ = x.shape
    x = x + 0.5 * swiglu(rmsnorm(x, w['g_ff1']), w['w_ff1_gate'], w['w_ff1_up'], w['w_ff1_down'])
    h = rmsnorm(x, w['g_mix'])
    qa, ka, va, qb, kb, vb, qc = jnp.split(h @ w['w_in'], SPLITS, axis=-1)
    heads = lambda t, n: t.reshape(b, s, n, HEAD_DIM)
    qa = rope(rmsnorm(heads(qa, A_Q_HEADS), w['g_qa']), pos)
    ka = rope(rmsnorm(heads(ka, A_KV_HEADS), w['g_ka']), pos)
    va = heads(va, A_KV_HEADS)
    qb = rmsnorm(heads(qb, B_HEADS), w['g_qb'])
    kb = rmsnorm(heads(kb, B_HEADS), w['g_kb'])
    vb = heads(vb, B_HEADS)
    qc = rmsnorm(heads(qc, C_HEADS), w['g_qc'])
    ya = attn_a(qa, ka, va)
    yb = attn_b(qb, kb, vb)
    yc = mem_attend(qc, mem_k, mem_v)
    gates = jax.nn.sigmoid((h @ w['w_gate'] + w['b_gate']).astype(jnp.float32))
    gates = gates.astype(x.dtype).reshape(b, s, N_BRANCH, D_MODEL)
    merged = (gates[..., 0, :] * (ya @ w['w_br_a']) + gates[..., 1, :] * (yb @ w['w_br_b'])
              + gates[..., 2, :] * (yc @ w['w_br_c']))
    x = x + merged @ w['w_out']
    x = x + 0.5 * swiglu(rmsnorm(x, w['g_ff2']), w['w_ff2_gate'], w['w_ff2_up'], w['w_ff2_down'])
    return rmsnorm(x, w['g_final']), (ka, va, kb, vb)


def setup_inputs(seed: int = 0) -> dict:
    key = jax.random.key(seed)
    ks = iter(jax.random.split(key, 48))
    nrm = lambda shape, scale: jax.random.normal(next(ks), shape, jnp.float32) * scale
    gain = lambda shape: 1.0 + nrm(shape, 0.01)
    L = DEPTH
    na = min(A_REACH, PAST_LEN)
    nb = min(B_REACH, PAST_LEN)
    return {
        'x_prompt': nrm((BATCH, SEQ, D_MODEL), 1.0),
        'x_sample': nrm((DEC_BATCH, DEC_SEQ, D_MODEL), 1.0),
        'cache_a_k': nrm((L, DEC_BATCH, na, A_KV_HEADS, HEAD_DIM), 1.0),
        'cache_a_v': nrm((L, DEC_BATCH, na, A_KV_HEADS, HEAD_DIM), 1.0),
        'cache_b_k': nrm((L, DEC_BATCH, nb, B_HEADS, HEAD_DIM), 1.0),
        'cache_b_v': nrm((L, DEC_BATCH, nb, B_HEADS, HEAD_DIM), 1.0),
        'cache_mem_k': nrm((L, DEC_BATCH, N_MEM, C_HEADS, HEAD_DIM), 1.0),
        'cache_mem_v': nrm((L, DEC_BATCH, N_MEM, C_HEADS, HEAD_DIM), 1.0),
        'mem_prompt': nrm((BATCH, N_MEM, D_MODEL), 1.0),
        'g_ff1': gain((L, D_MODEL)),
        'w_ff1_gate': nrm((L, D_MODEL, FF_DIM), D_MODEL ** -0.5),
        'w_ff1_up': nrm((L, D_MODEL, FF_DIM), D_MODEL ** -0.5),
        'w_ff1_down': nrm((L, FF_DIM, D_MODEL), FF_DIM ** -0.5),
        'g_mix': gain((L, D_MODEL)),
        'w_in': nrm((L, D_MODEL, IN_COLS), D_MODEL ** -0.5),
        'g_qa': gain((L, HEAD_DIM)),
        'g_ka': gain((L, HEAD_DIM)),
        'sinks_a': nrm((L, A_Q_HEADS), 0.5),
        'g_qb': gain((L, HEAD_DIM)),
        'g_kb': gain((L, HEAD_DIM)),
        'rel_bias_b': nrm((L, B_HEADS, 2 * REL_CLIP + 1), 0.1),
        'g_qc': gain((L, HEAD_DIM)),
        'g_mem': gain((L, D_MODEL)),
        'w_mem_kv': nrm((L, D_MODEL, 2 * C_W), D_MODEL ** -0.5),
        'g_kc': gain((L, HEAD_DIM)),
        'w_gate': nrm((L, D_MODEL, N_BRANCH * D_MODEL), D_MODEL ** -0.5),
        'b_gate': nrm((L, N_BRANCH * D_MODEL), 0.01),
        'w_br_a': nrm((L, A_Q, D_MODEL), A_Q ** -0.5),
        'w_br_b': nrm((L, B_W, D_MODEL), B_W ** -0.5),
        'w_br_c': nrm((L, C_W, D_MODEL), C_W ** -0.5),
        'w_out': nrm((L, D_MODEL, D_MODEL), D_MODEL ** -0.5),
        'g_ff2': gain((L, D_MODEL)),
        'w_ff2_gate': nrm((L, D_MODEL, FF_DIM), D_MODEL ** -0.5),
        'w_ff2_up': nrm((L, D_MODEL, FF_DIM), D_MODEL ** -0.5),
        'w_ff2_down': nrm((L, FF_DIM, D_MODEL), FF_DIM ** -0.5),
        'g_final': gain((L, D_MODEL)),
    }


def reference(x_prompt, x_sample, cache_a_k, cache_a_v, cache_b_k, cache_b_v, cache_mem_k, cache_mem_v,
              mem_prompt, g_ff1, w_ff1_gate, w_ff1_up, w_ff1_down, g_mix, w_in, g_qa, g_ka, sinks_a,
              g_qb, g_kb, rel_bias_b, g_qc, g_mem, w_mem_kv, g_kc, w_gate, b_gate, w_br_a, w_br_b,
              w_br_c, w_out, g_ff2, w_ff2_gate, w_ff2_up, w_ff2_down, g_final):
    s_p = x_prompt.shape[1]
    s_s = x_sample.shape[1]
    pos_p = jnp.arange(s_p, dtype=jnp.int32)
    pos_s = PAST_LEN + jnp.arange(s_s, dtype=jnp.int32)
    keep_a = min(A_REACH, s_p)
    keep_b = min(B_REACH, s_p)
    y_p, y_s = x_prompt, x_sample
    akp, avp, bkp, bvp, mkp, mvp, aks, avs, bks, bvs = ([] for _ in range(10))
    for l in range(DEPTH):
        w = dict(g_ff1=g_ff1[l], w_ff1_gate=w_ff1_gate[l], w_ff1_up=w_ff1_up[l], w_ff1_down=w_ff1_down[l],
                 g_mix=g_mix[l], w_in=w_in[l], g_qa=g_qa[l], g_ka=g_ka[l], g_qb=g_qb[l], g_kb=g_kb[l],
                 g_qc=g_qc[l], w_gate=w_gate[l], b_gate=b_gate[l], w_br_a=w_br_a[l], w_br_b=w_br_b[l],
                 w_br_c=w_br_c[l], w_out=w_out[l], g_ff2=g_ff2[l], w_ff2_gate=w_ff2_gate[l],
                 w_ff2_up=w_ff2_up[l], w_ff2_down=w_ff2_down[l], g_final=g_final[l])
        core_a = functools.partial(swa_sink_attend, sinks=sinks_a[l])
        core_b = functools.partial(chunk_relpos_attend, rel_bias=rel_bias_b[l])
        mk, mv = memory_kv(mem_prompt, g_mem[l], w_mem_kv[l], g_kc[l])
        y_p, (ka, va, kb, vb) = trunk_layer(
            y_p, pos_p, mk, mv,
            functools.partial(sweep_chunks, core_a, A_REACH),
            functools.partial(sweep_chunks, core_b, B_REACH), w)
        akp.append(ka[:, s_p - keep_a:]); avp.append(va[:, s_p - keep_a:])
        bkp.append(kb[:, s_p - keep_b:]); bvp.append(vb[:, s_p - keep_b:])
        mkp.append(mk); mvp.append(mv)
        y_s, (ka, va, kb, vb) = trunk_layer(
            y_s, pos_s, cache_mem_k[l], cache_mem_v[l],
            functools.partial(cached_attend, core_a, cache_a_k[l], cache_a_v[l]),
            functools.partial(cached_attend, core_b, cache_b_k[l], cache_b_v[l]), w)
        aks.append(ka); avs.append(va); bks.append(kb); bvs.append(vb)
    return (y_p, y_s, jnp.stack(akp), jnp.stack(avp), jnp.stack(bkp), jnp.stack(bvp),
            jnp.stack(mkp), jnp.stack(mvp), jnp.stack(aks), jnp.stack(avs), jnp.stack(bks), jnp.stack(bvs))
```

```python
import contextlib
import numpy as np
import concourse.bass as bass
import concourse.mybir as mybir
from concourse.bass_utils import run_bass_kernel_spmd

F32 = mybir.dt.float32
BF16 = mybir.dt.bfloat16
AF = mybir.ActivationFunctionType
ALU = mybir.AluOpType
AX = mybir.AxisListType

D = 1024
FF = 2816
NCH = 22
EPS = 1e-6
NSLOTS = 20
CONV_LOOKAHEAD = 14
PREFETCH_X = False
NCORES = 8

HALF_A = list(range(0, 11))
HALF_B = list(range(11, 22))


def tile_stream_blocks():
    seq = []
    for ff in ("ff1",):
        for half in (HALF_A, HALF_B):
            for c in half:
                seq.append((ff, "gate", c))
                seq.append((ff, "up", c))
            for c in half:
                seq.append((ff, "down", c))
    for g in range(4):
        for kk in range(4):
            seq.append(("win", g, kk))
    for m in range(8):
        for j in range(3):
            seq.append(("wgate", j, m))
        seq.append(("br", m))
    for k in range(8):
        seq.append(("wout", k))
    for half in (HALF_A, HALF_B):
        for c in half:
            seq.append(("ff2", "gate", c))
            seq.append(("ff2", "up", c))
        for c in half:
            seq.append(("ff2", "down", c))
    return seq


TILE_BLOCKS = tile_stream_blocks()
MEM_BLOCKS = [("wmem", kk) for kk in range(4)]
ALL_BLOCKS = TILE_BLOCKS + MEM_BLOCKS
BLK_ID = {b: i for i, b in enumerate(ALL_BLOCKS)}
NBLK = len(ALL_BLOCKS)


class Op:
    __slots__ = ("eng", "fn", "deps", "signals", "token", "dma", "ndma", "grp")

    def __init__(self, eng, fn, dma, ndma, grp):
        self.eng = eng
        self.fn = fn
        self.deps = []
        self.signals = False
        self.token = None
        self.dma = dma
        self.ndma = ndma
        self.grp = grp


class Prog:
    ENGS = ("pe", "act", "dve", "pool", "sp")

    def __init__(self):
        self.ops = {e: [] for e in self.ENGS}
        self.last_writer = {}
        self.readers = {}
        self.dma_keys = []
        self.dma_key_set = set()

    def op(self, eng, fn, reads=(), writes=(), dma=None, ndma=1, grp=None):
        o = Op(eng, fn, dma, ndma, grp)
        deps = []
        for r in reads:
            w = self.last_writer.get(r)
            if w is not None:
                deps.append(w)
            if isinstance(r, tuple) and r[0] == "ps":
                rd = self.readers.get(r)
                if rd:
                    for k_, o_ in rd.items():
                        if k_ != eng:
                            deps.append(o_)
        for w_ in writes:
            w = self.last_writer.get(w_)
            if w is not None and not (grp is not None and w.grp == grp and w.eng == eng):
                deps.append(w)
            rd = self.readers.get(w_)
            if rd:
                deps.extend(rd.values())
        seen = set()
        for d in deps:
            if id(d) in seen:
                continue
            seen.add(id(d))
            if d.eng == "pe" and eng == "pe" and d.dma is None and dma is None:
                continue
            o.deps.append(d)
            d.signals = True
        for r in reads:
            rd = self.readers.setdefault(r, {})
            if dma is None:
                rd[eng] = o
            else:
                rd[("dma", id(o))] = o
        for w_ in writes:
            self.last_writer[w_] = o
            self.readers[w_] = {}
        if dma is not None:
            o.signals = True
            if dma not in self.dma_key_set:
                self.dma_key_set.add(dma)
                self.dma_keys.append(dma)
        self.ops[eng].append(o)
        return o

    def emit(self, nc):
        with contextlib.ExitStack() as st:
            esem = {e: st.enter_context(nc.semaphore("s_" + e)) for e in self.ENGS}
            dsem = {k: st.enter_context(nc.semaphore("d%d" % i)) for i, k in enumerate(self.dma_keys)}
            for e in self.ENGS:
                cnt = 0
                for o in self.ops[e]:
                    if o.dma is None and o.signals:
                        cnt += 1
                        o.token = (esem[e], cnt)
            dcnt_all = {}
            for e in self.ENGS:
                for o in self.ops[e]:
                    if o.dma is not None:
                        v = dcnt_all.get(o.dma, 0) + 16 * o.ndma
                        dcnt_all[o.dma] = v
                        o.token = (dsem[o.dma], v)
            block = st.enter_context(nc.Block())

            def run(ename, e):
                waited = {}
                for o in self.ops[ename]:
                    need = {}
                    for d in o.deps:
                        s, v = d.token
                        k = id(s)
                        if waited.get(k, 0) >= v:
                            continue
                        if k not in need or need[k][1] < v:
                            need[k] = (s, v)
                    for k, (s, v) in need.items():
                        waited[k] = v
                    need = list(need.values())
                    for (s, v) in need[:-1]:
                        e.wait_ge(s, v)
                    ins = o.fn(e) if o.fn is not None else None
                    first = last = None
                    if ins is not None:
                        if isinstance(ins, (list, tuple)):
                            first, last = ins[0], ins[-1]
                        else:
                            first = last = ins
                    if need:
                        if first is not None:
                            first._wait_ge(*need[-1])
                        else:
                            e.wait_ge(*need[-1])
                    if o.token is not None and last is not None:
                        if o.dma is not None:
                            lst = ins if isinstance(ins, (list, tuple)) else [ins]
                            assert len(lst) == o.ndma
                            for i_ in lst:
                                i_.then_inc(o.token[0], 16)
                        else:
                            last.then_inc(o.token[0], 1)

            @block.tensor
            def _(e):
                run("pe", e)

            @block.scalar
            def _(e):
                run("act", e)

            @block.vector
            def _(e):
                run("dve", e)

            @block.gpsimd
            def _(e):
                run("pool", e)

            @block.sync
            def _(e):
                run("sp", e)


class _Stop(Exception):
    pass


def build_nc(SEQ, do_sample=True, stop=None):
    NT = SEQ // 512

    def chk(stage):
        if stop is not None and stage == stop:
            raise _Stop()
    nc = bass.Bass("TRN2", target_bir_lowering=False)
    P = Prog()
    st = contextlib.ExitStack()

    def din(name, shape):
        return nc.dram_tensor(name, list(shape), F32, kind="ExternalInput").ap()

    def dout(name, shape):
        return nc.dram_tensor(name, list(shape), F32, kind="ExternalOutput").ap()

    xp = din("xp", [2, SEQ, D])
    xs_in = din("xs", [128, D])
    cak = din("cak", [2, 128, 128]); cav = din("cav", [2, 128, 128])
    cbk = din("cbk", [2, 512, 256]); cbv = din("cbv", [2, 512, 256])
    cmk = din("cmk", [2, 256, 256]); cmv = din("cmv", [2, 256, 256])
    memp = din("memp", [2, 256, D])
    W = {}
    for ff in ("ff1", "ff2"):
        W[ff, "gate"] = din("w_%s_gate" % ff, [D, FF])
        W[ff, "up"] = din("w_%s_up" % ff, [D, FF])
        W[ff, "down"] = din("w_%s_down" % ff, [FF, D])
    w_in = din("w_in", [D, 1792])
    w_mem = din("w_mem", [D, 512])
    w_gate = din("w_gate", [D, 3072])
    w_bra = din("w_bra", [512, D]); w_brb = din("w_brb", [256, D]); w_brc = din("w_brc", [256, D])
    w_out = din("w_out", [D, D])
    gT_in = din("gT", [128, 4, 8])
    gfin_in = din("gfin", [1, D])
    gqk_in = din("gqk", [1, 1408])
    gkc_in = din("gkc", [1, 256])
    bgT_in = din("bgT", [128, 24])
    sinks_in = din("sinks", [1, 8])
    biasT_in = din("biasT", [128, 4, 3, 64])
    biasc_in = din("biasc", [128, 4])
    ropeP_in = din("ropeP", [SEQ, 64])
    ropeS_in = din("ropeS", [128, 64])
    ident_in = din("ident", [128, 128])
    zmat_in = din("zmat", [128, 192])

    yp = dout("yp", [2, SEQ, D])
    ys = dout("ys", [128, D])
    akp = dout("akp", [2, 128, 128]); avp = dout("avp", [2, 128, 128])
    bkp = dout("bkp", [2, 512, 256]); bvp = dout("bvp", [2, 512, 256])
    mkp = dout("mkp", [2, 256, 256]); mvp = dout("mvp", [2, 256, 256])
    aks = dout("aks", [128, 128]); avs = dout("avs", [128, 128])
    bks = dout("bks", [128, 256]); bvs = dout("bvs", [128, 256])

    wscr = nc.dram_tensor("wscr", [128, NBLK, 1024], BF16).ap()

    def sb(name, shape, dt):
        return st.enter_context(nc.sbuf_tensor(name, list(shape), dt))

    x = sb("x", [128, 4, D], F32)
    ystage = sb("ystage", [128, 2, D], F32)
    xsb = sb("xsb", [128, 2, D], BF16)
    junk = sb("junk", [128, D], BF16)
    hT = sb("hT", [128, 8, 512], BF16)
    big = sb("big", [128, 16, 512], BF16)
    qn = sb("qn", [128, 1408], F32)
    scr6 = sb("scr6", [128, 1536], F32)
    kadup2 = sb("kadup", [128, 2, 256], F32)
    vf32 = sb("vf32", [128, 384], F32)
    kaT = sb("kaT", [128, 2, 1024], BF16)
    kbT = sb("kbT", [128, 2, 1024], BF16)
    vA = sb("vA", [128, 8, 512], BF16)
    vB = sb("vB", [128, 8, 512], BF16)
    mkT = sb("mkT", [128, 2, 2, 256], BF16)
    vM = sb("vM", [128, 2, 2, 512], BF16)
    pA = sb("pA", [128, 2, 512], BF16)
    pB = sb("pB", [128, 2, 640], BF16)
    sig = sb("sig", [128, 3, 512], BF16)
    sgt = sig
    rec = sb("rec", [128, 768], F32)
    den = sb("den", [128, 512], F32)
    gfin = sb("gfin_s", [128, D], F32)
    gqk = sb("gqk_s", [128, 1408], F32)
    gkc = sb("gkc_s", [128, 256], F32)
    gT = sb("gT_s", [128, 4, 8], F32)
    bgT = sb("bgT_s", [128, 24], F32)
    sinkexp = sb("sinkexp", [128, 8], F32)
    biasc = sb("biasc_s", [128, 4], F32)
    biasb = sb("biasb", [128, 4, 3, 64], BF16)
    identb = sb("identb", [128, 128], F32)
    identb16 = sb("identb16", [128, 128], BF16)
    zmatb = sb("zmatb", [128, 192], BF16)
    epsc = sb("epsc", [128, 1], F32)
    stat = sb("stat", [128, 96], F32)
    rope = sb("rope", [128, 2, 4, 64], F32)
    stg32 = sb("stg32", [128, 2, 2048], F32)
    stg16 = sb("stg16", [128, 2, 2048], BF16)
    slots = sb("slots", [128, NSLOTS, 1024], BF16)
    ps = st.enter_context(nc.psum_tensor("ps", [128, 8, 512], F32))
    stg_flat = stg32[:].rearrange("p a c -> p (a c)")
    QN = [qn[:, :], stg_flat[:, 0:1408]]
    SCR = [scr6[:, :], stg_flat[:, 1408:2944]]
    KAD = [kadup2[:, 0, :], kadup2[:, 1, :]]
    stg16f = stg16[:].rearrange("p a c -> p (a c)").bitcast(F32)
    XALT = {1: stg_flat[:, 2944:3968], 2: stg16f[:, 0:1024], 3: stg16f[:, 1024:2048]}
    XCUR = {"alt": False}

    def XB(s, alt=None):
        a_ = XCUR["alt"] if alt is None else alt
        if s == 0 or not a_:
            return x[:, s, :], ("x", s)
        return XALT[s], ("xa", s)

    def psb(b):
        return ps[:, b, :].bitcast(BF16)

    def ld(dst, src, key, eng="pool"):
        P.op(eng, lambda e: e.dma_start(out=dst, in_=src), reads=(), writes=(key,), dma=key)

    ld(gT[:], gT_in, "c_gT")
    ld(gfin[:], gfin_in.partition_broadcast(128), "c_gfin")
    ld(gqk[:], gqk_in.partition_broadcast(128), "c_gqk")
    ld(gkc[:], gkc_in.partition_broadcast(128), "c_gkc")
    ld(bgT[:], bgT_in, "c_bgT")
    ld(sinkexp[:], sinks_in.partition_broadcast(128), "c_sink")
    biasf = scr6[:, 0:768].rearrange("p (h j q) -> p h j q", h=4, j=3)
    identf = scr6[:, 768:896]
    zmatf = scr6[:, 896:1088]
    ld(biasf, biasT_in, "c_biasf")
    ld(biasc[:], biasc_in, "c_biasc")
    ld(identf, ident_in, "c_identf")
    ld(zmatf, zmat_in, "c_zmatf")
    P.op("dve", lambda e: e.memset(epsc[:], EPS), writes=("c_eps",))
    P.op("dve", lambda e: e.tensor_copy(out=identb[:], in_=identf), reads=("c_identf",), writes=("c_identb",))
    P.op("dve", lambda e: e.tensor_copy(out=identb16[:], in_=identf), reads=("c_identf",), writes=("c_identb16",))
    P.op("dve", lambda e: e.tensor_copy(out=zmatb[:], in_=zmatf), reads=("c_zmatf",), writes=("c_zmatb",))
    P.op("act", lambda e: e.activation(out=sinkexp[:], in_=sinkexp[:], func=AF.Exp), reads=("c_sink",), writes=("c_sink",))
    bf3 = scr6[:, 0:768].rearrange("p (h c) -> p h c", h=4)
    P.op("dve", lambda e: e.tensor_tensor(out=bf3, in0=bf3, in1=biasc[:].unsqueeze(2).to_broadcast([128, 4, 192]), op=ALU.subtract),
         reads=("c_biasf", "c_biasc"), writes=("c_biasf",))
    P.op("dve", lambda e: e.tensor_scalar(out=biasb[:], in0=biasf, scalar1=8.0, scalar2=None, op0=ALU.mult),
         reads=("c_biasf",), writes=("c_biasb", ("scr6", 0)))
    P.op("pool", lambda e: e.memset(vA[:], 1.0), writes=tuple(("vA", t) for t in range(8)))
    P.op("pool", lambda e: e.memset(vB[:], 1.0), writes=tuple(("vB", t) for t in range(8)))
    P.op("pool", lambda e: e.memset(vM[:], 1.0), writes=(("vM", 0), ("vM", 1)))

    conv_units = []
    conv_state = {"n": 0, "i": 0}

    def add_unit(loads, casts, stores):
        conv_units.append((loads, casts, stores))

    def colblock_unit(srcs, col0, blks, gidx=None):
        def loads(sv):
            out = []
            k0 = 0
            for (src, nk) in srcs:
                v = sv.rearrange("p (k c) -> p k c", k=8)[:, k0:k0 + nk, :]
                out.append((v, src[:, col0:col0 + 256].rearrange("(k p) c -> p k c", p=128)))
                k0 += nk
            return out

        def casts(sv, dv):
            g_ = None
            if gidx is not None:
                g_ = gT[:, gidx, :].unsqueeze(1).unsqueeze(3).to_broadcast([128, 2, 8, 128])
            return [(dv.rearrange("p (m k j) -> p m k j", m=2, k=8),
                     sv.rearrange("p (k m j) -> p m k j", k=8, m=2), g_)]
        add_unit(loads, casts, blks)

    def rowblock_unit(src, r0, blks):
        def loads(sv):
            return [(sv.rearrange("p (k c) -> p k c", k=2), src[r0:r0 + 256, :].rearrange("(k p) c -> p k c", p=128))]

        def casts(sv, dv):
            return [(dv, sv, None)]
        add_unit(loads, casts, blks)

    def kpair_unit(src, k0, c0, wg, blks, gidx=None):
        def loads(sv):
            return [(sv.rearrange("p (k c) -> p k c", k=4)[:, :, 0:wg],
                     src[k0 * 128:(k0 + 4) * 128, c0:c0 + wg].rearrange("(k p) c -> p k c", p=128))]

        def casts(sv, dv):
            g_ = None
            if gidx is not None:
                g_ = gT[:, gidx, k0:k0 + 4].unsqueeze(2).to_broadcast([128, 4, wg])
            return [(dv.rearrange("p (k c) -> p k c", k=4)[:, :, 0:wg], sv.rearrange("p (k c) -> p k c", k=4)[:, :, 0:wg], g_)]
        add_unit(loads, casts, blks)

    def ff_units(ff):
        for c0 in range(0, NCH, 2):
            gi_ = 0 if ff == "ff1" else 2
            colblock_unit([(W[ff, "gate"], 8)], c0 * 128, [BLK_ID[(ff, "gate", c0)], BLK_ID[(ff, "gate", c0 + 1)]], gidx=gi_)
            colblock_unit([(W[ff, "up"], 8)], c0 * 128, [BLK_ID[(ff, "up", c0)], BLK_ID[(ff, "up", c0 + 1)]], gidx=gi_)
            rowblock_unit(W[ff, "down"], c0 * 128, [BLK_ID[(ff, "down", c0)], BLK_ID[(ff, "down", c0 + 1)]])

    for kk2 in range(2):
        kpair_unit(w_mem, kk2 * 4, 0, 512, [BLK_ID[("wmem", kk2 * 2)], BLK_ID[("wmem", kk2 * 2 + 1)]], gidx=3)
    ff_units("ff1")
    n_prologue_units = len(conv_units)
    WIN_G = [(0, 512), (512, 512), (1024, 384), (1408, 384)]
    for g, (c0, wg) in enumerate(WIN_G):
        for kk2 in range(2):
            kpair_unit(w_in, kk2 * 4, c0, wg, [BLK_ID[("win", g, kk2 * 2)], BLK_ID[("win", g, kk2 * 2 + 1)]], gidx=1)
    for m0 in range(0, 8, 2):
        for j in range(3):
            colblock_unit([(w_gate, 8)], (j * 8 + m0) * 128, [BLK_ID[("wgate", j, m0)], BLK_ID[("wgate", j, m0 + 1)]], gidx=1)
        colblock_unit([(w_bra, 4), (w_brb, 2), (w_brc, 2)], m0 * 128, [BLK_ID[("br", m0)], BLK_ID[("br", m0 + 1)]])
    for k0 in range(0, 8, 2):
        rowblock_unit(w_out, k0 * 128, [BLK_ID[("wout", k0)], BLK_ID[("wout", k0 + 1)]])
    ff_units("ff2")

    def conv_step(n):
        for _ in range(n):
            i = conv_state["i"]
            if i >= len(conv_units):
                return
            conv_state["i"] = i + 1
            par = i % 2
            loads, casts, stores = conv_units[i]
            sv = stg32[:, par, :]
            dv = stg16[:, par, :]
            lds = loads(sv)
            P.op("sp", (lambda lds=lds: (lambda e: [e.dma_start(out=dst, in_=src) for (dst, src) in lds]))(),
                 writes=(("stg32", par),), dma=("stg32", par), ndma=len(lds))
            engines = ("dve", "act")
            ceng = engines[i % len(engines)]
            for (o_, i_, g_) in casts(sv, dv):
                if g_ is not None:
                    P.op("dve", (lambda o_=o_, i_=i_, g_=g_: (lambda e: e.tensor_tensor(out=o_, in0=i_, in1=g_, op=ALU.mult)))(),
                         reads=(("stg32", par), "c_gT"), writes=(("stg16", par),))
                    continue
                if ceng == "act":
                    fn = (lambda o_=o_, i_=i_: (lambda e: e.activation(out=o_, in_=i_, func=AF.Copy)))()
                else:
                    fn = (lambda o_=o_, i_=i_: (lambda e: e.tensor_copy(out=o_, in_=i_)))()
                P.op(ceng, fn, reads=(("stg32", par),), writes=(("stg16", par),))
            P.op("pool", (lambda stores=stores, dv=dv: (lambda e: [e.dma_start(out=wscr[:, blk, :], in_=dv[:, bi * 1024:(bi + 1) * 1024])
                                                                   for bi, blk in enumerate(stores)]))(),
                 reads=(("stg16", par),), writes=tuple(("scr", blk) for blk in stores), dma=("stg16", par), ndma=len(stores))

    stream = []
    for seq in range(2):
        stream += MEM_BLOCKS
        for t in range(NT):
            stream += TILE_BLOCKS
    if do_sample:
        stream += TILE_BLOCKS
    ws = {"loaded": 0, "pos": 0, "rel": 0}

    def ws_record_loads():
        while ws["loaded"] < len(stream) and ws["loaded"] < ws["rel"] + NSLOTS:
            i = ws["loaded"]
            blk = BLK_ID[stream[i]]
            sl = i % NSLOTS
            for la in range(min(i + CONV_LOOKAHEAD, len(stream) - 1), i - 1, -1):
                if ("scr", BLK_ID[stream[la]]) not in P.last_writer:
                    while ("scr", BLK_ID[stream[la]]) not in P.last_writer:
                        assert conv_state["i"] < len(conv_units)
                        conv_step(1)
                    break
            P.op("sp", (lambda blk=blk, sl=sl: (lambda e: e.dma_start(out=slots[:, sl, :], in_=wscr[:, blk, :])))(),
                 reads=(("scr", blk),), writes=(("slot", sl),), dma=("slot", sl))
            ws["loaded"] += 1

    def ws_get(expect):
        i = ws["pos"]
        assert stream[i] == expect, (stream[i], expect)
        assert i < ws["loaded"], "weight stream deadlock: increase NSLOTS"
        ws["pos"] += 1
        return i % NSLOTS

    def ws_done(n):
        ws["rel"] += n
        ws_record_loads()

    bank_rr = {"ffgu": 0, "ffd": 0, "tr": 0, "wout": 0}

    def norm_A(xsrc, xkey, s, so=0):
        ss = stat[:, so + s:so + s + 1]
        sq_ = stat[:, so + 8 + s:so + 9 + s]
        P.op("act", lambda e: e.activation(out=junk[:], in_=xsrc, func=AF.Square, accum_out=ss),
             reads=(xkey,), writes=(("ss", so, s),))
        P.op("act", lambda e: e.activation(out=sq_, in_=ss, func=AF.Sqrt, bias=epsc[:, 0:1], scale=1.0 / D),
             reads=(("ss", so, s), "c_eps"), writes=(("sqr", so, s),))

    def norm_B(xsrc, xkey, gidx, s, col0):
        norm_B1(xsrc, xkey, s)
        norm_B2(gidx, s, col0)

    def norm_B1(xsrc, xkey, s, so=0):
        par = s % 2
        sq_ = stat[:, so + 8 + s:so + 9 + s]
        rstd = stat[:, so + 16 + s:so + 17 + s]
        P.op("dve", lambda e: e.reciprocal(out=rstd, in_=sq_), reads=(("sqr", so, s),), writes=(("rstd", so, s),))
        P.op("dve", lambda e: e.tensor_scalar(out=xsb[:, par, :], in0=xsrc, scalar1=rstd, scalar2=None, op0=ALU.mult),
             reads=(xkey, ("rstd", so, s)), writes=(("xsb", par),))

    def norm_B2(gidx, s, col0, bank=None):
        par = s % 2
        if bank is None:
            b = 6 if bank_rr["tr"] % 2 == 0 else 4
            bank_rr["tr"] += 1
        else:
            b = bank

        def tr(e):
            out = []
            for k in range(8):
                out.append(e.transpose(out=psb(b + k // 4)[:, (k % 4) * 128:(k % 4 + 1) * 128], in_=xsb[:, par, k * 128:(k + 1) * 128],
                                       identity=identb16[:]))
            return out
        P.op("pe", tr, reads=(("xsb", par), "c_identb16"), writes=(("ps", b), ("ps", b + 1)))
        sub = col0 // 128
        P.op("act", lambda e: e.activation(out=hT[:, 0:4, col0:col0 + 128], in_=psb(b)[:, 0:512].rearrange("p (k c) -> p k c", k=4),
                                           func=AF.Copy),
             reads=(("ps", b),), writes=tuple(("hT", k, sub) for k in range(4)))
        P.op("dve", lambda e: e.tensor_copy(out=hT[:, 4:8, col0:col0 + 128], in_=psb(b + 1)[:, 0:512].rearrange("p (k c) -> p k c", k=4)),
             reads=(("ps", b + 1),), writes=tuple(("hT", k, sub) for k in range(4, 8)))

    def norm_to_hT(xsrc, xkey, gidx, s, col0):
        norm_A(xsrc, xkey, s)
        norm_B(xsrc, xkey, gidx, s, col0)

    class NormPipe:
        def __init__(self, gidx, alt=None, so=0, bank=None):
            self.gidx = gidx
            self.pending = None
            self.pending2 = None
            self.alt = XCUR["alt"] if alt is None else alt
            self.so = so
            self.bank = bank

        def feed(self, s):
            xa_, xk_ = XB(s, self.alt)
            norm_A(xa_, xk_, s, self.so)
            if self.pending is not None:
                p_ = self.pending
                xa_, xk_ = XB(p_, self.alt)
                norm_B1(xa_, xk_, p_, self.so)
            if self.pending2 is not None:
                norm_B2(self.gidx, self.pending2, self.pending2 * 128, self.bank)
            self.pending2 = self.pending
            self.pending = s

        def flush_step1(self):
            if self.pending is not None:
                p_ = self.pending
                xa_, xk_ = XB(p_, self.alt)
                norm_B1(xa_, xk_, p_, self.so)
            if self.pending2 is not None:
                norm_B2(self.gidx, self.pending2, self.pending2 * 128, self.bank)
            self.pending2 = None

        def flush_step2(self):
            if self.pending is not None:
                norm_B2(self.gidx, self.pending, self.pending * 128, self.bank)
            self.pending = None

        def flush(self):
            self.flush_step1()
            self.flush_step2()

    class FinalPipe:
        def __init__(self, out_rows):
            self.out_rows = out_rows
            self.pending = None

        def feed(self, s):
            final_A(s)
            if self.pending is not None:
                final_B(self.pending, self.out_rows)
            self.pending = s

        def flush(self):
            if self.pending is not None:
                final_B(self.pending, self.out_rows)
                self.pending = None

    def hT_keys(ns):
        return tuple(("hT", k, s) for k in range(8) for s in range(ns))

    def big_keys(c, ns):
        return tuple(("big", c, s) for s in range(ns))

    def ffn(ff, NS, post_sub=None):
        TT = NS * 128
        for half in (HALF_A, HALF_B):
            for ci, c in enumerate(half):
                sg = ws_get((ff, "gate", c))
                su = ws_get((ff, "up", c))
                bg = (bank_rr["ffgu"] % 2) * 2
                bu = bg + 1
                bank_rr["ffgu"] += 1
                par = ci % 2

                def mm(e, sl, bk):
                    out = []
                    for k in range(8):
                        out.append(e.matmul(ps[:, bk, 0:TT], lhsT=slots[:, sl, k * 128:(k + 1) * 128], rhs=hT[:, k, 0:TT],
                                            start=(k == 0), stop=(k == 7)))
                    return out
                P.op("pe", (lambda sg=sg, bg=bg: (lambda e: mm(e, sg, bg)))(), reads=(("slot", sg),) + hT_keys(NS), writes=(("ps", bg),))
                P.op("pe", (lambda su=su, bu=bu: (lambda e: mm(e, su, bu)))(), reads=(("slot", su),) + hT_keys(NS), writes=(("ps", bu),))
                ws_done(2)
                P.op("act", (lambda bg=bg, par=par: (lambda e: e.activation(out=sgt[:, par, 0:TT], in_=ps[:, bg, 0:TT], func=AF.Silu)))(),
                     reads=(("ps", bg),), writes=(("sig", par),))
                P.op("dve", (lambda bu=bu, par=par, ci=ci: (lambda e: e.tensor_tensor(out=big[:, ci, 0:TT], in0=ps[:, bu, 0:TT],
                                                                                     in1=sgt[:, par, 0:TT], op=ALU.mult)))(),
                     reads=(("ps", bu), ("sig", par)), writes=big_keys(ci, NS))
            dsl = [ws_get((ff, "down", c)) for c in half]
            for s in range(NS):
                for n in range(2):
                    b = 4 + bank_rr["ffd"] % 2
                    bank_rr["ffd"] += 1

                    def mmd(e, s=s, n=n, b=b, dsl=dsl):
                        out = []
                        for ci in range(11):
                            out.append(e.matmul(ps[:, b, :], lhsT=big[:, ci, s * 128:(s + 1) * 128],
                                                rhs=slots[:, dsl[ci], n * 512:(n + 1) * 512], start=(ci == 0), stop=(ci == 10)))
                        return out
                    P.op("pe", mmd, reads=tuple(("slot", q) for q in dsl) + tuple(("big", ci, s) for ci in range(11)),
                         writes=(("ps", b),))
                    xa_, xk_ = XB(s)
                    P.op("dve", (lambda xa_=xa_, n=n, b=b: (lambda e: e.scalar_tensor_tensor(
                        out=xa_[:, n * 512:(n + 1) * 512], in0=ps[:, b, :], scalar=0.5, in1=xa_[:, n * 512:(n + 1) * 512],
                        op0=ALU.mult, op1=ALU.add)))(), reads=(("ps", b), xk_), writes=(xk_,))
                if half is HALF_B and post_sub is not None:
                    post_sub(s)
            ws_done(11)

    def headnorm(psrc_list, nheads, gtab, bp=0):
        W_ = nheads * 64
        qn_, scr_ = QN[bp], SCR[bp]
        qk, sk, ssk = ("qn", bp), ("scr6", bp), ("ssq", bp)
        sc0 = 24 + 24 * bp
        for (pap, c0, w, bk) in psrc_list:
            P.op("act", (lambda pap=pap, c0=c0, w=w: (lambda e: e.activation(out=scr_[:, c0:c0 + w], in_=pap, func=AF.Square)))(),
                 reads=(("ps", bk),), writes=(sk,), grp=("hnsq", bp))
        ssq = stat[:, sc0:sc0 + nheads]
        P.op("dve", lambda e: e.tensor_reduce(out=ssq, in_=scr_[:, 0:W_].rearrange("p (h d) -> p h d", d=64), axis=AX.X, op=ALU.add),
             reads=(sk,), writes=(ssk,))
        P.op("act", lambda e: e.activation(out=ssq, in_=ssq, func=AF.Sqrt, bias=epsc[:, 0:1], scale=1.0 / 64),
             reads=(ssk, "c_eps"), writes=(ssk,))
        P.op("dve", lambda e: e.reciprocal(out=ssq, in_=ssq), reads=(ssk,), writes=(ssk,))
        for (pap, c0, w, bk) in psrc_list:
            nh = w // 64
            h0 = c0 // 64
            P.op("dve", (lambda pap=pap, c0=c0, w=w, nh=nh, h0=h0: (lambda e: e.tensor_tensor(
                out=qn_[:, c0:c0 + w].rearrange("p (h d) -> p h d", d=64), in0=pap.rearrange("p (h d) -> p h d", d=64),
                in1=stat[:, sc0 + h0:sc0 + h0 + nh].unsqueeze(2).to_broadcast([128, nh, 64]), op=ALU.mult)))(),
                reads=(("ps", bk), ssk), writes=(qk,), grp=("hnp1", bp))
        P.op("pool", lambda e: e.tensor_tensor(out=qn_[:, 0:W_], in0=qn_[:, 0:W_], in1=gtab, op=ALU.mult),
             reads=(qk, "c_gqk", "c_gkc"), writes=(qk,))

    def vaug_write(dst_tile_ap, src_ap, nh_src, reads, wkey, eng_cycle=("act",)):
        d4 = dst_tile_ap.rearrange("p (a b c) -> p a b c", a=2, b=2)
        if nh_src == 2:
            s3 = src_ap.rearrange("p (k d) -> p k d", d=64)
            pairs = [(d4[:, :, 0, 0:64], s3), (d4[:, :, 1, 64:128], s3)]
        else:
            s4 = src_ap.rearrange("p (a b d) -> p a b d", a=2, b=2)
            pairs = [(d4[:, :, 0, 0:64], s4[:, :, 0, :]), (d4[:, :, 1, 64:128], s4[:, :, 1, :])]
        for i, (o_, i_) in enumerate(pairs):
            eng = eng_cycle[i % len(eng_cycle)]
            if eng == "act":
                fn = (lambda o_=o_, i_=i_: (lambda e: e.activation(out=o_, in_=i_, func=AF.Copy)))()
            else:
                fn = (lambda o_=o_, i_=i_: (lambda e: e.tensor_copy(out=o_, in_=i_)))()
            P.op(eng, fn, reads=reads, writes=(wkey,))

    def transposes_to(srcs, dests, reads, banks):
        def tr(e):
            out = []
            for t, src in enumerate(srcs):
                out.append(e.transpose(out=ps[:, banks[t // 4], (t % 4) * 128:(t % 4 + 1) * 128], in_=src, identity=identb[:]))
            return out
        nb = (len(srcs) + 3) // 4
        P.op("pe", tr, reads=reads + ("c_identb",), writes=tuple(("ps", banks[i]) for i in range(nb)))
        for i, (t0, n, oap, wkeys) in enumerate(dests):
            assert t0 // 4 == (t0 + n - 1) // 4
            bk = banks[t0 // 4]
            iap = ps[:, bk, (t0 % 4) * 128:(t0 % 4 + n) * 128].rearrange("p (n c) -> p n c", n=n)
            eng = "act" if i % 2 == 0 else "dve"
            if eng == "act":
                fn = (lambda oap=oap, iap=iap: (lambda e: e.activation(out=oap, in_=iap, func=AF.Copy)))()
            else:
                fn = (lambda oap=oap, iap=iap: (lambda e: e.tensor_copy(out=oap, in_=iap)))()
            P.op(eng, fn, reads=(("ps", bk),), writes=wkeys)

    def keylist(slots_j):
        by_t = {}
        for (sl, j) in slots_j:
            t = sl // 2
            ent = by_t.setdefault(t, [False, False, None, None])
            ent[sl % 2] = True
            ent[2 + sl % 2] = j
        return [(t % 8, v[0], v[1], v[2], v[3]) for t, v in sorted(by_t.items())]

    def attend_chunk(qc0, klA, klB):
        ycols = slice(qc0, qc0 + 64)
        sidx = qc0 // 128
        nA = len(klA)
        nB = len(klB)

        class _St:
            pass
        stg = _St()

        def sa(e):
            out = []
            for i, (t, lo, hi, _, _) in enumerate(klA):
                for kvg in range(2):
                    for par_ in range(2):
                        hp = par_ * 64
                        pp0 = kvg * 2
                        c_ = i * 256 + pp0 * 64
                        out.append(e.matmul(ps[:, par_, c_:c_ + 128].rearrange("p (a q) -> p a q", a=2),
                                            lhsT=kaT[hp:hp + 64, kvg, t * 128:(t + 1) * 128],
                                            rhs=big[hp:hp + 64, pp0:pp0 + 2, ycols], start=True, stop=True))
            return out
        def stage_sa():
            P.op("pe", sa, reads=tuple(("kaT", t) for (t, _, _, _, _) in klA) + tuple(("big", c, sidx) for c in range(4)),
                 writes=(("ps", 0), ("ps", 1)))
            P.op("act", lambda e: e.activation(out=pA[:, :, 0:nA * 256], in_=ps[:, 0:2, 0:nA * 256], func=AF.Exp, scale=0.125),
                 reads=(("ps", 0), ("ps", 1)), writes=("pA",))
        stg.sa = stage_sa

        def sbm(e):
            out = []
            for i, (t, lo, hi, jlo, jhi) in enumerate(klB):
                bl = lo and jlo is not None and jlo <= 2
                bh = hi and jhi is not None and jhi <= 2
                for par_ in range(2):
                    hp = par_ * 64
                    c_ = i * 128
                    bk = 2 + par_ * 2 + c_ // 512
                    reg = ps[:, bk, c_ % 512:c_ % 512 + 128]
                    started = False
                    if bl:
                        out.append(e.matmul(reg.rearrange("p (a q) -> p a q", a=2), lhsT=zmatb[hp:hp + 64, 64:192],
                                            rhs=biasb[hp:hp + 64, par_:4:2, jlo, :], start=True, stop=False))
                        started = True
                    if bh:
                        out.append(e.matmul(reg.rearrange("p (a q) -> p a q", a=2), lhsT=zmatb[hp:hp + 64, 0:128],
                                            rhs=biasb[hp:hp + 64, par_:4:2, jhi, :], start=(not started), stop=False))
                        started = True
                    for hh in range(2):
                        out.append(e.matmul(reg[:, hh * 64:(hh + 1) * 64], lhsT=kbT[hp:hp + 64, hh, t * 128:(t + 1) * 128],
                                            rhs=big[hp:hp + 64, 4 + hh, ycols], start=(not started), stop=True))
            return out
        bankB = (("ps", 2), ("ps", 3), ("ps", 4), ("ps", 5))
        psB = ps[:, 2:6, :].rearrange("p (par b) c -> p par (b c)", par=2)

        def stage_sb():
            P.op("pe", sbm, reads=tuple(("kbT", t) for (t, _, _, _, _) in klB) + tuple(("big", 4 + c, sidx) for c in range(2))
                 + ("c_zmatb", "c_biasb"), writes=bankB)
            P.op("act", lambda e: e.activation(out=pB[:, :, 0:nB * 128], in_=psB[:, :, 0:nB * 128], func=AF.Exp, scale=0.125),
                 reads=bankB, writes=("pB",))
        stg.sb = stage_sb

        def prange(lo, hi):
            if lo and hi:
                return 0, 128
            return (0, 64) if lo else (64, 128)

        def pva(e):
            out = []
            for kvg in range(2):
                for par_ in range(2):
                    blk = kvg * 2 + par_
                    pp0 = kvg * 2
                    for i, (t, lo, hi, _, _) in enumerate(klA):
                        p0, p1 = prange(lo, hi)
                        c_ = i * 256 + pp0 * 64
                        oc = par_ * 256 + pp0 * 64
                        out.append(e.matmul(ps[:, 6, oc:oc + 128], lhsT=vA[p0:p1, t, blk * 128:(blk + 1) * 128],
                                            rhs=pA[p0:p1, par_, c_:c_ + 128], start=(i == 0), stop=(i == nA - 1)))
            return out

        def pvb(e):
            out = []
            for h in range(4):
                for i, (t, lo, hi, _, _) in enumerate(klB):
                    p0, p1 = prange(lo, hi)
                    c_ = i * 128 + (h // 2) * 64
                    out.append(e.matmul(ps[:, 7, h * 64:(h + 1) * 64], lhsT=vB[p0:p1, t, h * 128:(h + 1) * 128],
                                        rhs=pB[p0:p1, h % 2, c_:c_ + 64], start=(i == 0), stop=(i == nB - 1)))
            return out
        oA = ps[:, 6, :].rearrange("p (two h q) -> p h two q", two=2, q=64)
        sx = sinkexp[:].rearrange("p (h two) -> p h two", two=2)
        denA = den[:, 0:512].rearrange("p (h two q) -> p h two q", two=2, q=64)
        recA = rec[:, 0:512].rearrange("p (h two q) -> p h two q", two=2, q=64)
        def stage_pva():
          P.op("pe", pva, reads=("pA",) + tuple(("vA", t) for (t, _, _, _, _) in klA), writes=(("ps", 6),))
          for par, (np0, dp0) in enumerate(((0, 64), (64, 0))):
            P.op("dve", (lambda par=par, dp0=dp0: (lambda e: e.tensor_tensor(
                out=denA[dp0:dp0 + 64, :, par, :], in0=oA[dp0:dp0 + 64, :, par, :],
                in1=sx[dp0:dp0 + 64, :, par].unsqueeze(2).to_broadcast([64, 4, 64]), op=ALU.add)))(),
                reads=(("ps", 6), "c_sink"), writes=(("den", par),))
            P.op("dve", (lambda par=par, dp0=dp0, np0=np0: (lambda e: e.reciprocal(
                out=recA[np0:np0 + 64, :, par, :], in_=denA[dp0:dp0 + 64, :, par, :])))(),
                reads=(("den", par),), writes=(("rec", par),))
            P.op("dve", (lambda par=par, np0=np0: (lambda e: e.tensor_tensor(
                out=big[np0:np0 + 64, 8:12, ycols], in0=oA[np0:np0 + 64, :, par, :], in1=recA[np0:np0 + 64, :, par, :],
                op=ALU.mult)))(),
                reads=(("ps", 6), ("rec", par)), writes=tuple(("big", 8 + c, sidx) for c in range(4)), grp="yT")
        stg.pva = stage_pva
        oB = ps[:, 7, 0:256].rearrange("p (h two q) -> p h two q", two=2, q=64)
        recB = rec[:, 512:768].rearrange("p (h two q) -> p h two q", two=2, q=64)

        oBs = vf32[:, 0:256].rearrange("p (h two q) -> p h two q", two=2, q=64)

        def stage_pvb():
          P.op("pe", pvb, reads=("pB",) + tuple(("vB", t) for (t, _, _, _, _) in klB), writes=(("ps", 7),))
          P.op("act", lambda e: e.activation(out=vf32[:, 0:256], in_=ps[:, 7, 0:256], func=AF.Copy), reads=(("ps", 7),), writes=("vf32",))
          for par, (np0, dp0) in enumerate(((0, 64), (64, 0))):
            P.op("dve", (lambda par=par, dp0=dp0, np0=np0: (lambda e: e.reciprocal(
                out=recB[np0:np0 + 64, :, par, :], in_=oBs[dp0:dp0 + 64, :, par, :])))(),
                reads=("vf32",), writes=(("recB", par),))
            P.op("pool", (lambda par=par, np0=np0: (lambda e: e.tensor_tensor(
                out=big[np0:np0 + 64, 12:14, ycols], in0=oBs[np0:np0 + 64, :, par, :], in1=recB[np0:np0 + 64, :, par, :],
                op=ALU.mult)))(),
                reads=("vf32", ("recB", par)), writes=tuple(("big", 12 + c, sidx) for c in range(2)), grp="yTB")
        stg.pvb = stage_pvb
        return stg

    def attend_mem(c0, n, mi):
        sidxs = sorted(set(range(c0 // 128, (c0 + n + 127) // 128)))
        pCs = [(pB[:, :, 0:512], "pB"), (pA[:, :, :], "pA")]

        def scores(h):
            hp = (h % 2) * 64
            b0 = (h % 2) * 2
            pC, pkey = pCs[h % 2]

            def sc(e):
                out = []
                for t in range(2):
                    out.append(e.matmul(ps[:, b0 + t, 0:n], lhsT=mkT[hp:hp + 64, mi, h // 2, t * 128:(t + 1) * 128],
                                        rhs=big[hp:hp + 64, 6 + h // 2, c0:c0 + n], start=True, stop=True))
                return out
            P.op("pe", sc, reads=(("mkT", mi),) + tuple(("big", 6 + h // 2, s) for s in sidxs), writes=(("ps", b0), ("ps", b0 + 1)))
            P.op("act", lambda e: e.activation(out=pC[:, :, 0:n], in_=ps[:, b0:b0 + 2, 0:n], func=AF.Exp, scale=0.125),
                 reads=(("ps", b0), ("ps", b0 + 1)), writes=(pkey,))

        def pvn(h):
            pC, pkey = pCs[h % 2]
            bo = 6 + h % 2
            rk = ("recC", h % 2)
            rbuf = rec[:, 0:512] if h % 2 == 0 else den[:, 0:512]

            def pv(e):
                out = []
                for t in range(2):
                    out.append(e.matmul(ps[:, bo, 0:n], lhsT=vM[:, mi, t, h * 128:(h + 1) * 128], rhs=pC[:, t, 0:n],
                                        start=(t == 0), stop=(t == 1)))
                return out
            P.op("pe", pv, reads=(pkey, ("vM", mi)), writes=(("ps", bo),))
            np0, dp0 = ((0, 64), (64, 0))[h % 2]
            P.op("dve", lambda e: e.reciprocal(out=rbuf[np0:np0 + 64, 0:n], in_=ps[dp0:dp0 + 64, bo, 0:n]),
                 reads=(("ps", bo), ("rec", 0), ("rec", 1), ("den", 0), ("den", 1)), writes=(rk, ("rec", 0), ("rec", 1), ("den", 0), ("den", 1)))
            P.op("dve", lambda e: e.tensor_tensor(out=big[np0:np0 + 64, 14 + h // 2, c0:c0 + n],
                                                  in0=ps[np0:np0 + 64, bo, 0:n], in1=rbuf[np0:np0 + 64, 0:n], op=ALU.mult),
                 reads=(("ps", bo), rk), writes=tuple(("big", 14 + h // 2, s) for s in sidxs), grp="yT")

        scores(0)
        for h in range(1, 4):
            scores(h)
            pvn(h - 1)
        pvn(3)

    GTAB_QK = gqk[:, 0:1408]

    def mixer(NS, tile_info):
        TT = NS * 128
        kind = tile_info["kind"]
        rpar = tile_info["rpar"]
        if kind == "prompt":
            ti = tile_info["ti"]
            P.op("pool", lambda e: e.dma_start(out=rope[:, rpar, :, :], in_=ropeP_in[ti * 512:(ti + 1) * 512, :].rearrange("(s p) c -> p s c", p=128)),
                 writes=(("rope", rpar),), dma=("rope", rpar))
        else:
            P.op("pool", lambda e: e.dma_start(out=rope[:, rpar, 0, :], in_=ropeS_in), writes=(("rope", rpar),), dma=("rope", rpar))
        win_sl = {}
        for g in range(4):
            for kk in range(4):
                win_sl[g, kk] = ws_get(("win", g, kk))
        dbl = tile_info.get("dbl", False)

        def chain(s):
            bset = 4 * (s % 2)
            bp = (s % 2) if dbl else 0
            qn_, scr_, kad_ = QN[bp], SCR[bp], KAD[bp]
            qk, sk = ("qn", bp), ("scr6", bp)

            def proj(e):
                out = []
                for k in range(8):
                    for g, (c0, wg) in enumerate(WIN_G):
                        out.append(e.matmul(ps[:, bset + g, 0:wg], lhsT=hT[:, k, s * 128:(s + 1) * 128],
                                            rhs=slots[:, win_sl[g, k // 2], (k % 2) * 512:(k % 2) * 512 + wg],
                                            start=(k == 0), stop=(k == 7)))
                return out
            P.op("pe", proj, reads=tuple(("slot", v) for v in win_sl.values()) + tuple(("hT", k, s) for k in range(8)),
                 writes=tuple(("ps", bset + g) for g in range(4)))
            if s == NS - 1:
                ws_done(16)
            headnorm([(ps[:, bset + 0, :], 0, 512, bset + 0), (ps[:, bset + 1, :], 512, 512, bset + 1),
                      (ps[:, bset + 2, 0:384], 1024, 384, bset + 2)], 22, GTAB_QK, bp)
            q4 = qn_[:, 0:640].rearrange("p (h two d) -> p h two d", two=2, d=32)
            x1 = q4[:, :, 0, :]
            x2 = q4[:, :, 1, :]
            rs_ = s if kind == "prompt" else 0
            cosb = rope[:, rpar, rs_, 0:32].unsqueeze(1).to_broadcast([128, 10, 32])
            sinb = rope[:, rpar, rs_, 32:64].unsqueeze(1).to_broadcast([128, 10, 32])
            tmp = scr_[:, 0:1280].rearrange("p (a h d) -> p a h d", a=4, d=32)

            def tt(o_, a_, b_, op_):
                return lambda e: e.tensor_tensor(out=o_, in0=a_, in1=b_, op=op_)
            gname = ("ropetmp", bp)
            P.op("pool", tt(tmp[:, 0], x1, cosb, ALU.mult), reads=(qk, ("rope", rpar)), writes=(sk,), grp=gname)
            P.op("pool", tt(tmp[:, 1], x2, sinb, ALU.mult), reads=(qk, ("rope", rpar)), writes=(sk,), grp=gname)
            P.op("pool", tt(tmp[:, 2], x1, sinb, ALU.mult), reads=(qk, ("rope", rpar)), writes=(sk,), grp=gname)
            P.op("pool", tt(tmp[:, 3], x2, cosb, ALU.mult), reads=(qk, ("rope", rpar)), writes=(sk,), grp=gname)
            P.op("pool", tt(x1, tmp[:, 0], tmp[:, 1], ALU.subtract), reads=(sk,), writes=(qk,))
            P.op("pool", tt(x2, tmp[:, 2], tmp[:, 3], ALU.add), reads=(sk,), writes=(qk,))
            if kind == "prompt":
                seq, ti = tile_info["seq"], tile_info["ti"]
                gt = (ti * 4 + s) % 8
                last = (ti == NT - 1)
            else:
                seq = None
                gt = 0
                last = True
            if last:
                P.op("act", lambda e: e.activation(out=vf32[:], in_=ps[:, bset + 3, 0:384], func=AF.Copy),
                     reads=(("ps", bset + 3),), writes=("vf32",))
            vaug_write(vA[:, gt, :], ps[:, bset + 3, 0:128], 2, (("ps", bset + 3),), ("vA", gt))
            vaug_write(vB[:, gt, :], ps[:, bset + 3, 128:384], 4, (("ps", bset + 3),), ("vB", gt))
            if last:
                dq = ("qn", bp)
                if kind == "prompt":
                    r0 = s * 128
                    P.op("pool", lambda e: e.dma_start(out=bkp[seq, r0:r0 + 128, :], in_=qn_[:, 896:1152]),
                         reads=(qk,), writes=(("o_bkp", seq, s),), dma=dq)
                    P.op("pool", lambda e: e.dma_start(out=bvp[seq, r0:r0 + 128, :], in_=vf32[:, 128:384]),
                         reads=("vf32",), writes=(("o_bvp", seq, s),), dma="vf32")
                    if s == NS - 1:
                        P.op("pool", lambda e: e.dma_start(out=akp[seq, :, :], in_=qn_[:, 512:640]),
                             reads=(qk,), writes=(("o_akp", seq),), dma=dq)
                        P.op("pool", lambda e: e.dma_start(out=avp[seq, :, :], in_=vf32[:, 0:128]),
                             reads=("vf32",), writes=(("o_avp", seq),), dma="vf32")
                else:
                    P.op("pool", lambda e: e.dma_start(out=bks, in_=qn_[:, 896:1152]), reads=(qk,), writes=("o_bks",), dma=dq)
                    P.op("pool", lambda e: e.dma_start(out=bvs, in_=vf32[:, 128:384]), reads=("vf32",), writes=("o_bvs",), dma="vf32")
                    P.op("pool", lambda e: e.dma_start(out=aks, in_=qn_[:, 512:640]), reads=(qk,), writes=("o_aks",), dma=dq)
                    P.op("pool", lambda e: e.dma_start(out=avs, in_=vf32[:, 0:128]), reads=("vf32",), writes=("o_avs",), dma="vf32")
            P.op("pool", lambda e: e.tensor_copy(out=kad_.rearrange("p (k two d) -> p k two d", two=2, d=64),
                                                 in_=qn_[:, 512:640].rearrange("p (k d) -> p k d", d=64).unsqueeze(2).to_broadcast([128, 2, 2, 64])),
                 reads=(qk,), writes=(("kadup", bp),))

        def trans(s):
            bset = 4 * (s % 2)
            bp = (s % 2) if dbl else 0
            qn_, kad_ = QN[bp], KAD[bp]
            if kind == "prompt":
                gt = (tile_info["ti"] * 4 + s) % 8
            else:
                gt = 0
            rc = gt * 128
            srcs = [qn_[:, t * 128:(t + 1) * 128] for t in range(4)] + [kad_[:, 0:128], kad_[:, 128:256]] + \
                   [qn_[:, 640 + t * 128:640 + (t + 1) * 128] for t in range(6)]
            transposes_to(srcs, [
                (0, 4, big[:, 0:4, s * 128:(s + 1) * 128], tuple(("big", c, s) for c in range(4))),
                (4, 2, kaT[:, :, rc:rc + 128], (("kaT", gt),)),
                (6, 2, big[:, 4:6, s * 128:(s + 1) * 128], tuple(("big", c, s) for c in (4, 5))),
                (8, 2, kbT[:, :, rc:rc + 128], (("kbT", gt),)),
                (10, 2, big[:, 6:8, s * 128:(s + 1) * 128], tuple(("big", c, s) for c in (6, 7))),
            ], (("qn", bp), ("kadup", bp)), [bset + 0, bset + 1, bset + 2])

        att = {"prev": None}

        def attn(cl):
            c = tile_info["ti"] * 8 + cl
            klA = keylist([(kc, c - kc) for kc in range(max(0, c - 2), c + 1)])
            klB = keylist([(kc, c - kc) for kc in range(max(0, c - 8), c + 1)])
            stg_ = attend_chunk(cl * 64, klA, klB)
            stg_.sa()
            if att["prev"] is not None:
                att["prev"].pvb()
            stg_.sb()
            stg_.pva()
            att["prev"] = stg_

        npipe = tile_info.get("npipe")
        if dbl and kind == "prompt":
            chain(0)
            chain(1)
            npipe.flush_step1()
            trans(0)
            attn(0); attn(1)
            chain(2)
            npipe.flush_step2()
            trans(1)
            attn(2); attn(3)
            chain(3)
            trans(2)
            attn(4); attn(5)
            trans(3)
            attn(6); attn(7)
            att["prev"].pvb()
            attend_mem(0, 512, 0)
        elif kind == "prompt":
            npipe.flush()
            for s in range(NS):
                chain(s)
                trans(s)
            for cl in range(8):
                attn(cl)
            att["prev"].pvb()
            attend_mem(0, 512, 0)
        else:
            for s in range(NS):
                chain(s)
                trans(s)
        if False:
            pass
        else:
            pass

    def post_attention(NS, post_sub=None):
        TT = NS * 128
        for m in range(8):
            gs = [ws_get(("wgate", j, m)) for j in range(3)]
            bs_ = ws_get(("br", m))
            for j in range(3):
                def mmg(e, j=j, sl=gs[j]):
                    out = []
                    for k in range(8):
                        out.append(e.matmul(ps[:, j, 0:TT], lhsT=slots[:, sl, k * 128:(k + 1) * 128], rhs=hT[:, k, 0:TT],
                                            start=(k == 0), stop=(k == 7)))
                    return out
                P.op("pe", mmg, reads=(("slot", gs[j]),) + hT_keys(NS), writes=(("ps", j),))
                P.op("act", (lambda j=j, m=m: (lambda e: e.activation(out=sig[:, j, 0:TT], in_=ps[:, j, 0:TT], func=AF.Sigmoid,
                                                                       bias=bgT[:, j * 8 + m:j * 8 + m + 1], scale=1.0)))(),
                     reads=(("ps", j), "c_bgT"), writes=(("sig", j),))
            ycs = [(8, 4, 0), (12, 2, 512), (14, 2, 768)]
            for j, (yc0, nk, off) in enumerate(ycs):
                def mmb(e, j=j, yc0=yc0, nk=nk, off=off, bs_=bs_):
                    out = []
                    for k in range(nk):
                        out.append(e.matmul(ps[:, 3 + j, 0:TT], lhsT=slots[:, bs_, off + k * 128:off + (k + 1) * 128],
                                            rhs=big[:, yc0 + k, 0:TT], start=(k == 0), stop=(k == nk - 1)))
                    return out
                rk = tuple(("big", yc0 + k, s) for k in range(nk) for s in range(NS))
                P.op("pe", mmb, reads=(("slot", bs_),) + rk, writes=(("ps", 3 + j),))
            ws_done(4)
            tj = scr6[:].rearrange("p (j c) -> p j c", j=3)
            for j in range(3):
                P.op("dve", (lambda j=j: (lambda e: e.tensor_tensor(out=tj[:, j, 0:TT], in0=ps[:, 3 + j, 0:TT], in1=sig[:, j, 0:TT], op=ALU.mult)))(),
                     reads=(("ps", 3 + j), ("sig", j)), writes=(("tj", j), ("scr6", 0)) if j == 0 else (("tj", j),))
            P.op("pool", lambda e: e.tensor_tensor(out=tj[:, 0, 0:TT], in0=tj[:, 0, 0:TT], in1=tj[:, 1, 0:TT], op=ALU.add),
                 reads=(("tj", 0), ("tj", 1)), writes=(("tj", 0),))
            P.op("pool", (lambda m=m: (lambda e: e.tensor_tensor(out=big[:, m, 0:TT], in0=tj[:, 0, 0:TT], in1=tj[:, 2, 0:TT], op=ALU.add)))(),
                 reads=(("tj", 0), ("tj", 2)), writes=big_keys(m, NS) + (("scr6", 0),))
        osl = [ws_get(("wout", k)) for k in range(8)]
        for s in range(NS):
            for n in range(2):
                b = 6 + bank_rr["wout"] % 2
                bank_rr["wout"] += 1

                def mmo(e, s=s, n=n, b=b):
                    out = []
                    for k in range(8):
                        out.append(e.matmul(ps[:, b, :], lhsT=big[:, k, s * 128:(s + 1) * 128], rhs=slots[:, osl[k], n * 512:(n + 1) * 512],
                                            start=(k == 0), stop=(k == 7)))
                    return out
                P.op("pe", mmo, reads=tuple(("slot", q) for q in osl) + tuple(("big", k, s) for k in range(8)), writes=(("ps", b),))
                xa_, xk_ = XB(s)
                P.op("dve", (lambda xa_=xa_, n=n, b=b: (lambda e: e.tensor_tensor(out=xa_[:, n * 512:(n + 1) * 512], in0=ps[:, b, :],
                                                                                 in1=xa_[:, n * 512:(n + 1) * 512], op=ALU.add)))(),
                     reads=(("ps", b), xk_), writes=(xk_,))
            if post_sub is not None:
                post_sub(s)
        ws_done(8)

    def final_A(s):
        ss = stat[:, s:s + 1]
        sq_ = stat[:, 8 + s:9 + s]
        xa_, xk_ = XB(s)
        P.op("act", lambda e: e.activation(out=junk[:], in_=xa_, func=AF.Square, accum_out=ss),
             reads=(xk_,), writes=(("ss", 0, s),))
        P.op("act", lambda e: e.activation(out=sq_, in_=ss, func=AF.Sqrt, bias=epsc[:, 0:1], scale=1.0 / D),
             reads=(("ss", 0, s), "c_eps"), writes=(("sqr", 0, s),))

    def final_B(s, out_rows):
        par = s % 2
        sq_ = stat[:, 8 + s:9 + s]
        rstd = stat[:, 16 + s:17 + s]
        xa_, xk_ = XB(s)
        P.op("dve", lambda e: e.reciprocal(out=rstd, in_=sq_), reads=(("sqr", 0, s),), writes=(("rstd", 0, s),))
        P.op("dve", lambda e: e.scalar_tensor_tensor(out=ystage[:, par, :], in0=xa_, scalar=rstd, in1=gfin[:],
                                                     op0=ALU.mult, op1=ALU.mult),
             reads=(xk_, ("rstd", 0, s), "c_gfin"), writes=(("ystage", par),))
        dst = out_rows(s)
        P.op("pool", lambda e: e.dma_start(out=dst, in_=ystage[:, par, :]),
             reads=(("ystage", par),), writes=(("o_y", id(dst)),), dma=("ystage", par))
        keep_alive.append(dst)

    def mem_phase(seq):
        msl = [ws_get(("wmem", kk)) for kk in range(4)]
        for j in range(2):
            P.op("pool", (lambda j=j: (lambda e: e.dma_start(out=ystage[:, j, :], in_=memp[seq, j * 128:(j + 1) * 128, :])))(),
                 writes=(("ystage", j),), dma=("ystage", j))
            chk(1.1)
            norm_to_hT(ystage[:, j, :], ("ystage", j), 3, j, j * 128)
            chk(1.2)
            b = 4 + j

            def mm(e, j=j, b=b):
                out = []
                for k in range(8):
                    out.append(e.matmul(ps[:, b, :], lhsT=hT[:, k, j * 128:(j + 1) * 128], rhs=slots[:, msl[k // 2], (k % 2) * 512:(k % 2) * 512 + 512],
                                        start=(k == 0), stop=(k == 7)))
                return out
            P.op("pe", mm, reads=tuple(("slot", q) for q in msl) + tuple(("hT", k, j) for k in range(8)), writes=(("ps", b),))
            chk(1.3)
            headnorm([(ps[:, b, 0:256], 0, 256, b)], 4, gkc[:], 0)
            chk(1.4)
            P.op("pool", (lambda j=j: (lambda e: e.dma_start(out=mkp[seq, j * 128:(j + 1) * 128, :], in_=qn[:, 0:256])))(),
                 reads=(("qn", 0),), writes=(("o_mkp", seq, j),), dma=("qn", 0))
            P.op("act", (lambda b=b: (lambda e: e.activation(out=vf32[:, 0:256], in_=ps[:, b, 256:512], func=AF.Copy)))(),
                 reads=(("ps", b),), writes=("vf32",))
            P.op("pool", (lambda j=j: (lambda e: e.dma_start(out=mvp[seq, j * 128:(j + 1) * 128, :], in_=vf32[:, 0:256])))(),
                 reads=("vf32",), writes=(("o_mvp", seq, j),), dma="vf32")
            chk(1.5)
            vaug_write(vM[:, 0, j, :], ps[:, b, 256:512], 4, (("ps", b),), ("vM", 0))
            chk(1.6)
            transposes_to([qn[:, 0:128], qn[:, 128:256]], [(0, 2, mkT[:, 0, :, j * 128:(j + 1) * 128], (("mkT", 0),))], (("qn", 0),), [6 + j])
        ws_done(4)

    def load_rows_T(src_ap, ncols, dup, dest_ap, wkey, bank):
        P.op("pool", lambda e: e.dma_start(out=qn[:, 0:ncols], in_=src_ap), writes=(("qn", 0),), dma="qn_in")
        kad_ = KAD[0]
        if dup:
            assert ncols == 128
            P.op("act", lambda e: e.activation(out=kad_.rearrange("p (k two d) -> p k two d", two=2, d=64),
                                               in_=qn[:, 0:128].rearrange("p (k d) -> p k d", d=64).unsqueeze(2).to_broadcast([128, 2, 2, 64]),
                                               func=AF.Copy), reads=(("qn", 0),), writes=(("kadup", 0),))
            srcs = [kad_[:, 0:128], kad_[:, 128:256]]
        else:
            srcs = [qn[:, t * 128:(t + 1) * 128] for t in range(ncols // 128)]
        transposes_to(srcs, [(0, len(srcs), dest_ap, (wkey,))], (("qn", 0), ("kadup", 0)), [bank])

    def load_rows_V(src_ap, nh_src, dst_tile_ap, wkey):
        w = nh_src * 64
        P.op("pool", lambda e: e.dma_start(out=vf32[:, 0:w], in_=src_ap), writes=("vf32",), dma="vf32_in")
        vaug_write(dst_tile_ap, vf32[:, 0:w], nh_src, ("vf32",), wkey)

    rpar_ctr = [0]
    keep_alive = []

    def x_load(seq, ti, subs=(0, 1, 2, 3), alt=None):
        for s in subs:
            r0 = ti * 512 + s * 128
            xa_, xk_ = XB(s, alt)
            P.op("pool", (lambda xa_=xa_, r0=r0: (lambda e: e.dma_start(out=xa_, in_=xp[seq, r0:r0 + 128, :])))(),
                 writes=(xk_,), dma=xk_)

    def main_program():
      chk(0)
      ws_record_loads()
      chk(1)
      tiles = [(seq, ti) for seq in range(2) for ti in range(NT)]
      prefetched = False
      for gi, (seq, ti) in enumerate(tiles):
            if ti == 0:
                mem_phase(seq)
                chk(2)
            first = (gi == 0)
            XCUR["alt"] = (gi >= 2 and gi % 2 == 0)
            if not prefetched:
                x_load(seq, ti)
                pp = NormPipe(0)
                for s in range(4):
                    pp.feed(s)
                pp.flush()
            chk(3)
            pp = NormPipe(1)
            ffn("ff1", 4, post_sub=pp.feed)
            chk(4)
            info = dict(kind="prompt", seq=seq, ti=ti, rpar=rpar_ctr[0] % 2, dbl=not first, npipe=pp)
            rpar_ctr[0] += 1
            mixer(4, info)
            chk(5)
            pp = NormPipe(2)
            post_attention(4, post_sub=pp.feed)
            pp.flush()
            chk(6)
            fin = FinalPipe((lambda seq=seq, ti=ti: (lambda s: yp[seq, ti * 512 + s * 128: ti * 512 + (s + 1) * 128, :]))())
            nxt = tiles[gi + 1] if gi + 1 < len(tiles) else None
            do_pf = PREFETCH_X and nxt is not None and nxt[1] != 0 and gi + 1 >= 2
            if do_pf:
                nalt = ((gi + 1) % 2 == 0)
                assert nalt != XCUR["alt"]
                x_load(nxt[0], nxt[1], subs=(1, 2, 3), alt=nalt)
                npipe2 = NormPipe(0, alt=nalt, so=72, bank=6)
                order = {0: 1, 1: 2, 2: 3}

                def post(s, fin=fin, npipe2=npipe2, order=order):
                    fin.feed(s)
                    if s in order:
                        npipe2.feed(order[s])
                ffn("ff2", 4, post_sub=post)
                fin.flush()
                x_load(nxt[0], nxt[1], subs=(0,), alt=nalt)
                npipe2.feed(0)
                npipe2.flush()
                prefetched = True
            else:
                ffn("ff2", 4, post_sub=fin.feed)
                fin.flush()
                prefetched = False
            chk(7)
      chk(8)
      if do_sample:
        sample_program()
      assert ws["pos"] == len(stream), (ws["pos"], len(stream))

    def sample_program():
        P.op("pool", lambda e: e.dma_start(out=x[:, 0, :], in_=xs_in), writes=(("x", 0),), dma=("x", 0))
        norm_to_hT(x[:, 0, :], ("x", 0), 0, 0, 0)
        ffn("ff1", 1)
        norm_to_hT(x[:, 0, :], ("x", 0), 1, 0, 0)
        info = dict(kind="sample", rpar=rpar_ctr[0] % 2, dbl=False)
        mixer(1, info)
        for sseq in range(2):
            for j in range(2):
                load_rows_T(cmk[sseq, j * 128:(j + 1) * 128, :], 256, False, mkT[:, sseq, :, j * 128:(j + 1) * 128], ("mkT", sseq), 6 + j)
                load_rows_V(cmv[sseq, j * 128:(j + 1) * 128, :], 4, vM[:, sseq, j, :], ("vM", sseq))
            load_rows_T(cak[sseq], 128, True, kaT[:, :, 7 * 128:8 * 128], ("kaT", 7), 6)
            load_rows_V(cav[sseq], 2, vA[:, 7, :], ("vA", 7))
            for j in range(4):
                load_rows_T(cbk[sseq, j * 128:(j + 1) * 128, :], 256, False, kbT[:, :, (4 + j) * 128:(5 + j) * 128], ("kbT", 4 + j), 6 + j % 2)
                load_rows_V(cbv[sseq, j * 128:(j + 1) * 128, :], 4, vB[:, 4 + j, :], ("vB", 4 + j))
            own = 16 + sseq
            klA = keylist([(14, 2), (15, 1), (own, 0)])
            klB = keylist([(kc, 16 - kc) for kc in range(8, 16)] + [(own, 0)])
            stg_ = attend_chunk(sseq * 64, klA, klB)
            stg_.sa()
            stg_.sb()
            stg_.pva()
            stg_.pvb()
            attend_mem(sseq * 64, 64, sseq)
        post_attention(1)
        norm_to_hT(x[:, 0, :], ("x", 0), 2, 0, 0)
        ffn("ff2", 1)
        final_A(0)
        final_B(0, lambda s: ys)

    try:
        main_program()
    except _Stop:
        pass
    okeys = tuple(k for k in P.last_writer.keys() if isinstance(k, tuple) and isinstance(k[0], str) and k[0].startswith("o_")) + \
        tuple(k for k in P.last_writer.keys() if isinstance(k, str) and k.startswith("o_"))
    fin = P.op("pool", None, reads=okeys)
    for en in P.ENGS:
        cand = [o for o in P.ops[en] if o is not fin and o.fn is not None]
        if cand:
            lo = cand[-1]
            lo.signals = True
            if lo not in fin.deps:
                fin.deps.append(lo)
    P.emit(nc)
    st.close()
    return nc


def _consts(SEQ):
    half = 32
    inv = (10000.0 ** (-np.arange(half, dtype=np.float32) / half)).astype(np.float32)

    def tab(pos):
        ang = pos.astype(np.float32)[:, None] * inv[None, :]
        return np.concatenate([np.cos(ang), np.sin(ang)], axis=1).astype(np.float32)
    ropeP = tab(np.arange(SEQ))
    ropeS = tab(1024 + (np.arange(128) % 64))
    ident = np.eye(128, dtype=np.float32)
    zmat = np.zeros((128, 192), np.float32)
    zmat[0:64, 64:128] = np.eye(64, dtype=np.float32)
    zmat[64:128, 64:128] = np.eye(64, dtype=np.float32)
    return ropeP, ropeS, ident, zmat


def _prep_shared(inp, SEQ):
    L = 0
    f = lambda a: np.ascontiguousarray(a, dtype=np.float32)
    w_in = inp["w_in"][L]
    perm = np.concatenate([np.arange(0, 512), np.arange(512, 640), np.arange(768, 1024), np.arange(1024, 1280),
                           np.arange(1536, 1792), np.arange(640, 768), np.arange(1280, 1536)])
    sh = {}
    sh["w_ff1_gate"] = f(inp["w_ff1_gate"][L])
    sh["w_ff1_up"] = f(inp["w_ff1_up"][L])
    sh["w_ff1_down"] = f(inp["w_ff1_down"][L])
    sh["w_ff2_gate"] = f(inp["w_ff2_gate"][L])
    sh["w_ff2_up"] = f(inp["w_ff2_up"][L])
    sh["w_ff2_down"] = f(inp["w_ff2_down"][L])
    sh["w_in"] = f(w_in[:, perm])
    sh["w_mem"] = f(inp["w_mem_kv"][L])
    sh["w_gate"] = f(inp["w_gate"][L])
    sh["w_bra"] = f(inp["w_br_a"][L]); sh["w_brb"] = f(inp["w_br_b"][L]); sh["w_brc"] = f(inp["w_br_c"][L])
    sh["w_out"] = f(inp["w_out"][L])
    gs = np.stack([inp["g_ff1"][L], inp["g_mix"][L], inp["g_ff2"][L], inp["g_mem"][L]], 0)
    sh["gT"] = f(gs.reshape(4, 8, 128).transpose(2, 0, 1))
    sh["gfin"] = f(inp["g_final"][L].reshape(1, D))
    gqk = np.concatenate([np.tile(inp["g_qa"][L], 8), np.tile(inp["g_ka"][L], 2), np.tile(inp["g_qb"][L], 4),
                          np.tile(inp["g_kb"][L], 4), np.tile(inp["g_qc"][L], 4)])
    sh["gqk"] = f(gqk.reshape(1, 1408))
    sh["gkc"] = f(np.tile(inp["g_kc"][L], 4).reshape(1, 256))
    sh["bgT"] = f(inp["b_gate"][L].reshape(24, 128).T)
    sh["sinks"] = f(inp["sinks_a"][L].reshape(1, 8))
    rb = inp["rel_bias_b"][L]
    kk = np.arange(64)[:, None, None]
    jj = np.arange(3)[None, :, None]
    qq = np.arange(64)[None, None, :]
    idx = np.clip(64 * jj + qq - kk, -128, 128) + 128
    bt = rb[:, idx].transpose(1, 0, 2, 3)
    sh["biasT"] = f(np.concatenate([bt, bt], axis=0))
    sh["biasc"] = f(np.tile(rb[:, 256].reshape(1, 4), (128, 1)))
    ropeP, ropeS, ident, zmat = _consts(SEQ)
    sh["ropeP"] = ropeP; sh["ropeS"] = ropeS; sh["ident"] = ident; sh["zmat"] = zmat
    return sh


def _core_inputs(inp, sh, c, SEQ):
    f = lambda a: np.ascontiguousarray(a, dtype=np.float32)
    b0 = 2 * c
    m = dict(sh)
    m["xp"] = f(inp["x_prompt"][b0:b0 + 2, :SEQ])
    m["xs"] = f(inp["x_sample"][b0:b0 + 2].reshape(128, D))
    m["cak"] = f(inp["cache_a_k"][0, b0:b0 + 2].reshape(2, 128, 128))
    m["cav"] = f(inp["cache_a_v"][0, b0:b0 + 2].reshape(2, 128, 128))
    m["cbk"] = f(inp["cache_b_k"][0, b0:b0 + 2].reshape(2, 512, 256))
    m["cbv"] = f(inp["cache_b_v"][0, b0:b0 + 2].reshape(2, 512, 256))
    m["cmk"] = f(inp["cache_mem_k"][0, b0:b0 + 2].reshape(2, 256, 256))
    m["cmv"] = f(inp["cache_mem_v"][0, b0:b0 + 2].reshape(2, 256, 256))
    m["memp"] = f(inp["mem_prompt"][b0:b0 + 2])
    return m


_NC_CACHE = {}


def run_cores(inp, SEQ, ncores, trace=False):
    if SEQ not in _NC_CACHE:
        _NC_CACHE[SEQ] = build_nc(SEQ)
    nc = _NC_CACHE[SEQ]
    sh = _prep_shared(inp, SEQ)
    in_maps = [_core_inputs(inp, sh, c, SEQ) for c in range(ncores)]
    res = run_bass_kernel_spmd(nc, in_maps, core_ids=list(range(ncores)), trace=trace)
    return res


def assemble(results, SEQ, ncores):
    B = 2 * ncores
    R = results
    cat = lambda name, shp: np.concatenate([np.asarray(r[name]).reshape(shp) for r in R], axis=0)
    y_p = cat("yp", (2, SEQ, D))
    y_s = cat("ys", (2, 64, D))
    akp = cat("akp", (2, 128, 2, 64))[None]
    avp = cat("avp", (2, 128, 2, 64))[None]
    bkp = cat("bkp", (2, 512, 4, 64))[None]
    bvp = cat("bvp", (2, 512, 4, 64))[None]
    mkp = cat("mkp", (2, 256, 4, 64))[None]
    mvp = cat("mvp", (2, 256, 4, 64))[None]
    aks = cat("aks", (2, 64, 2, 64))[None]
    avs = cat("avs", (2, 64, 2, 64))[None]
    bks = cat("bks", (2, 64, 4, 64))[None]
    bvs = cat("bvs", (2, 64, 4, 64))[None]
    return tuple(np.ascontiguousarray(a, dtype=np.float32) for a in
                 (y_p, y_s, akp, avp, bkp, bvp, mkp, mvp, aks, avs, bks, bvs))


def kernel(**inputs):
    inp = {k: np.asarray(v) for k, v in inputs.items()}
    SEQ = inp["x_prompt"].shape[1]
    res = run_cores(inp, SEQ, NCORES)
    return assemble(res.results, SEQ, NCORES)
```

```python
import contextlib
import numpy as np
import concourse.bass as bass
import concourse.mybir as mybir
from concourse.bass_utils import run_bass_kernel_spmd

F32 = mybir.dt.float32
BF16 = mybir.dt.bfloat16
AF = mybir.ActivationFunctionType
ALU = mybir.AluOpType
AX = mybir.AxisListType

D = 1024
FF = 2816
NCH = 22
EPS = 1e-6
NSLOTS = 20
CONV_LOOKAHEAD = 20
PREFETCH_X = False
NCORES = 8

HALF_A = list(range(0, 11))
HALF_B = list(range(11, 22))


def tile_stream_blocks():
    seq = []
    for ff in ("ff1",):
        for half in (HALF_A, HALF_B):
            for c in half:
                seq.append((ff, "gate", c))
                seq.append((ff, "up", c))
            for c in half:
                seq.append((ff, "down", c))
    for g in range(4):
        for kk in range(4):
            seq.append(("win", g, kk))
    for m in range(8):
        for j in range(3):
            seq.append(("wgate", j, m))
        seq.append(("br", m))
    for k in range(8):
        seq.append(("wout", k))
    for half in (HALF_A, HALF_B):
        for c in half:
            seq.append(("ff2", "gate", c))
            seq.append(("ff2", "up", c))
        for c in half:
            seq.append(("ff2", "down", c))
    return seq


TILE_BLOCKS = tile_stream_blocks()
MEM_BLOCKS = [("wmem", kk) for kk in range(4)]
ALL_BLOCKS = TILE_BLOCKS + MEM_BLOCKS
BLK_ID = {b: i for i, b in enumerate(ALL_BLOCKS)}
NBLK = len(ALL_BLOCKS)


class Op:
    __slots__ = ("eng", "fn", "deps", "signals", "token", "dma", "ndma", "grp")

    def __init__(self, eng, fn, dma, ndma, grp):
        self.eng = eng
        self.fn = fn
        self.deps = []
        self.signals = False
        self.token = None
        self.dma = dma
        self.ndma = ndma
        self.grp = grp


class Prog:
    ENGS = ("pe", "act", "dve", "pool", "sp")

    def __init__(self):
        self.ops = {e: [] for e in self.ENGS}
        self.last_writer = {}
        self.readers = {}
        self.dma_keys = []
        self.dma_key_set = set()

    def op(self, eng, fn, reads=(), writes=(), dma=None, ndma=1, grp=None):
        o = Op(eng, fn, dma, ndma, grp)
        deps = []
        for r in reads:
            w = self.last_writer.get(r)
            if w is not None:
                deps.append(w)
            if isinstance(r, tuple) and r[0] == "ps":
                rd = self.readers.get(r)
                if rd:
                    for k_, o_ in rd.items():
                        if k_ != eng:
                            deps.append(o_)
        for w_ in writes:
            w = self.last_writer.get(w_)
            if w is not None and not (grp is not None and w.grp == grp and w.eng == eng):
                deps.append(w)
            rd = self.readers.get(w_)
            if rd:
                deps.extend(rd.values())
        seen = set()
        for d in deps:
            if id(d) in seen:
                continue
            seen.add(id(d))
            if d.eng == "pe" and eng == "pe" and d.dma is None and dma is None:
                continue
            o.deps.append(d)
            d.signals = True
        for r in reads:
            rd = self.readers.setdefault(r, {})
            if dma is None:
                rd[eng] = o
            else:
                rd[("dma", id(o))] = o
        for w_ in writes:
            self.last_writer[w_] = o
            self.readers[w_] = {}
        if dma is not None:
            o.signals = True
            if dma not in self.dma_key_set:
                self.dma_key_set.add(dma)
                self.dma_keys.append(dma)
        self.ops[eng].append(o)
        return o

    def emit(self, nc):
        with contextlib.ExitStack() as st:
            esem = {e: st.enter_context(nc.semaphore("s_" + e)) for e in self.ENGS}
            dsem = {k: st.enter_context(nc.semaphore("d%d" % i)) for i, k in enumerate(self.dma_keys)}
            for e in self.ENGS:
                cnt = 0
                for o in self.ops[e]:
                    if o.dma is None and o.signals:
                        cnt += 1
                        o.token = (esem[e], cnt)
            dcnt_all = {}
            for e in self.ENGS:
                for o in self.ops[e]:
                    if o.dma is not None:
                        v = dcnt_all.get(o.dma, 0) + 16 * o.ndma
                        dcnt_all[o.dma] = v
                        o.token = (dsem[o.dma], v)
            block = st.enter_context(nc.Block())

            def run(ename, e):
                waited = {}
                for o in self.ops[ename]:
                    need = {}
                    for d in o.deps:
                        s, v = d.token
                        k = id(s)
                        if waited.get(k, 0) >= v:
                            continue
                        if k not in need or need[k][1] < v:
                            need[k] = (s, v)
                    for k, (s, v) in need.items():
                        waited[k] = v
                    need = list(need.values())
                    for (s, v) in need[:-1]:
                        e.wait_ge(s, v)
                    ins = o.fn(e) if o.fn is not None else None
                    first = last = None
                    if ins is not None:
                        if isinstance(ins, (list, tuple)):
                            first, last = ins[0], ins[-1]
                        else:
                            first = last = ins
                    if need:
                        if first is not None:
                            first._wait_ge(*need[-1])
                        else:
                            e.wait_ge(*need[-1])
                    if o.token is not None and last is not None:
                        if o.dma is not None:
                            lst = ins if isinstance(ins, (list, tuple)) else [ins]
                            assert len(lst) == o.ndma
                            for i_ in lst:
                                i_.then_inc(o.token[0], 16)
                        else:
                            last.then_inc(o.token[0], 1)

            @block.tensor
            def _(e):
                run("pe", e)

            @block.scalar
            def _(e):
                run("act", e)

            @block.vector
            def _(e):
                run("dve", e)

            @block.gpsimd
            def _(e):
                run("pool", e)

            @block.sync
            def _(e):
                run("sp", e)


class _Stop(Exception):
    pass


def build_nc(SEQ, do_sample=True, stop=None):
    NT = SEQ // 512

    def chk(stage):
        if stop is not None and stage == stop:
            raise _Stop()
    nc = bass.Bass("TRN2", target_bir_lowering=False)
    P = Prog()
    st = contextlib.ExitStack()

    def din(name, shape):
        return nc.dram_tensor(name, list(shape), F32, kind="ExternalInput").ap()

    def dout(name, shape):
        return nc.dram_tensor(name, list(shape), F32, kind="ExternalOutput").ap()

    xp = din("xp", [2, SEQ, D])
    xs_in = din("xs", [128, D])
    cak = din("cak", [2, 128, 128]); cav = din("cav", [2, 128, 128])
    cbk = din("cbk", [2, 512, 256]); cbv = din("cbv", [2, 512, 256])
    cmk = din("cmk", [2, 256, 256]); cmv = din("cmv", [2, 256, 256])
    memp = din("memp", [2, 256, D])
    W = {}
    for ff in ("ff1", "ff2"):
        W[ff, "gate"] = din("w_%s_gate" % ff, [D, FF])
        W[ff, "up"] = din("w_%s_up" % ff, [D, FF])
        W[ff, "down"] = din("w_%s_down" % ff, [FF, D])
    w_in = din("w_in", [D, 1792])
    w_mem = din("w_mem", [D, 512])
    w_gate = din("w_gate", [D, 3072])
    w_bra = din("w_bra", [512, D]); w_brb = din("w_brb", [256, D]); w_brc = din("w_brc", [256, D])
    w_out = din("w_out", [D, D])
    gT_in = din("gT", [128, 4, 8])
    gfin_in = din("gfin", [1, D])
    gqk_in = din("gqk", [1, 1408])
    gkc_in = din("gkc", [1, 256])
    bgT_in = din("bgT", [128, 24])
    sinks_in = din("sinks", [1, 8])
    biasT_in = din("biasT", [128, 4, 3, 64])
    biasc_in = din("biasc", [128, 4])
    ropeP_in = din("ropeP", [SEQ, 64])
    ropeS_in = din("ropeS", [128, 64])
    ident_in = din("ident", [128, 128])
    zmat_in = din("zmat", [128, 192])

    yp = dout("yp", [2, SEQ, D])
    ys = dout("ys", [128, D])
    akp = dout("akp", [2, 128, 128]); avp = dout("avp", [2, 128, 128])
    bkp = dout("bkp", [2, 512, 256]); bvp = dout("bvp", [2, 512, 256])
    mkp = dout("mkp", [2, 256, 256]); mvp = dout("mvp", [2, 256, 256])
    aks = dout("aks", [128, 128]); avs = dout("avs", [128, 128])
    bks = dout("bks", [128, 256]); bvs = dout("bvs", [128, 256])

    wscr = nc.dram_tensor("wscr", [128, NBLK, 1024], BF16).ap()

    def sb(name, shape, dt):
        return st.enter_context(nc.sbuf_tensor(name, list(shape), dt))

    x = sb("x", [128, 4, D], F32)
    ystage = sb("ystage", [128, 2, D], F32)
    xsb = sb("xsb", [128, 2, D], BF16)
    junk = sb("junk", [128, D], BF16)
    hT = sb("hT", [128, 8, 512], BF16)
    big = sb("big", [128, 16, 512], BF16)
    qn = sb("qn", [128, 1408], F32)
    scr6 = sb("scr6", [128, 1536], F32)
    kadup2 = sb("kadup", [128, 2, 256], F32)
    vf32 = sb("vf32", [128, 384], F32)
    kaT = sb("kaT", [128, 2, 1024], BF16)
    kbT = sb("kbT", [128, 2, 1024], BF16)
    vA = sb("vA", [128, 8, 512], BF16)
    vB = sb("vB", [128, 8, 512], BF16)
    mkT = sb("mkT", [128, 2, 2, 256], BF16)
    vM = sb("vM", [128, 2, 2, 512], BF16)
    pA = sb("pA", [128, 2, 512], BF16)
    pB = sb("pB", [128, 2, 640], BF16)
    sig = sb("sig", [128, 3, 512], BF16)
    sgt = sig
    rec = sb("rec", [128, 768], F32)
    den = sb("den", [128, 512], F32)
    gfin = sb("gfin_s", [128, D], F32)
    gqk = sb("gqk_s", [128, 1408], F32)
    gkc = sb("gkc_s", [128, 256], F32)
    gT = sb("gT_s", [128, 4, 8], F32)
    bgT = sb("bgT_s", [128, 24], F32)
    sinkexp = sb("sinkexp", [128, 8], F32)
    biasc = sb("biasc_s", [128, 4], F32)
    biasb = sb("biasb", [128, 4, 3, 64], BF16)
    identb = sb("identb", [128, 128], F32)
    identb16 = sb("identb16", [128, 128], BF16)
    zmatb = sb("zmatb", [128, 192], BF16)
    epsc = sb("epsc", [128, 1], F32)
    stat = sb("stat", [128, 96], F32)
    rope = sb("rope", [128, 2, 4, 64], F32)
    stg32 = sb("stg32", [128, 2, 2048], F32)
    stg16 = sb("stg16", [128, 2, 2048], BF16)
    slots = sb("slots", [128, NSLOTS, 1024], BF16)
    ps = st.enter_context(nc.psum_tensor("ps", [128, 8, 512], F32))
    stg_flat = stg32[:].rearrange("p a c -> p (a c)")
    QN = [qn[:, :], stg_flat[:, 0:1408]]
    SCR = [scr6[:, :], stg_flat[:, 1408:2944]]
    KAD = [kadup2[:, 0, :], kadup2[:, 1, :]]
    stg16f = stg16[:].rearrange("p a c -> p (a c)").bitcast(F32)
    XALT = {1: stg_flat[:, 2944:3968], 2: stg16f[:, 0:1024], 3: stg16f[:, 1024:2048]}
    XCUR = {"alt": False}

    def XB(s, alt=None):
        a_ = XCUR["alt"] if alt is None else alt
        if s == 0 or not a_:
            return x[:, s, :], ("x", s)
        return XALT[s], ("xa", s)

    def psb(b):
        return ps[:, b, :].bitcast(BF16)

    def ld(dst, src, key, eng="pool"):
        P.op(eng, lambda e: e.dma_start(out=dst, in_=src), reads=(), writes=(key,), dma=key)

    ld(gT[:], gT_in, "c_gT")
    ld(gfin[:], gfin_in.partition_broadcast(128), "c_gfin")
    ld(gqk[:], gqk_in.partition_broadcast(128), "c_gqk")
    ld(gkc[:], gkc_in.partition_broadcast(128), "c_gkc")
    ld(bgT[:], bgT_in, "c_bgT")
    ld(sinkexp[:], sinks_in.partition_broadcast(128), "c_sink")
    biasf = scr6[:, 0:768].rearrange("p (h j q) -> p h j q", h=4, j=3)
    identf = scr6[:, 768:896]
    zmatf = scr6[:, 896:1088]
    ld(biasf, biasT_in, "c_biasf")
    ld(biasc[:], biasc_in, "c_biasc")
    ld(identf, ident_in, "c_identf")
    ld(zmatf, zmat_in, "c_zmatf")
    P.op("dve", lambda e: e.memset(epsc[:], EPS), writes=("c_eps",))
    P.op("dve", lambda e: e.tensor_copy(out=identb[:], in_=identf), reads=("c_identf",), writes=("c_identb",))
    P.op("dve", lambda e: e.tensor_copy(out=identb16[:], in_=identf), reads=("c_identf",), writes=("c_identb16",))
    P.op("dve", lambda e: e.tensor_copy(out=zmatb[:], in_=zmatf), reads=("c_zmatf",), writes=("c_zmatb",))
    P.op("act", lambda e: e.activation(out=sinkexp[:], in_=sinkexp[:], func=AF.Exp), reads=("c_sink",), writes=("c_sink",))
    bf3 = scr6[:, 0:768].rearrange("p (h c) -> p h c", h=4)
    P.op("dve", lambda e: e.tensor_tensor(out=bf3, in0=bf3, in1=biasc[:].unsqueeze(2).to_broadcast([128, 4, 192]), op=ALU.subtract),
         reads=("c_biasf", "c_biasc"), writes=("c_biasf",))
    P.op("dve", lambda e: e.tensor_scalar(out=biasb[:], in0=biasf, scalar1=8.0, scalar2=None, op0=ALU.mult),
         reads=("c_biasf",), writes=("c_biasb", ("scr6", 0)))
    P.op("pool", lambda e: e.memset(vA[:], 1.0), writes=tuple(("vA", t) for t in range(8)))
    P.op("pool", lambda e: e.memset(vB[:], 1.0), writes=tuple(("vB", t) for t in range(8)))
    P.op("pool", lambda e: e.memset(vM[:], 1.0), writes=(("vM", 0), ("vM", 1)))

    conv_units = []
    conv_state = {"n": 0, "i": 0}

    def add_unit(loads, casts, stores):
        conv_units.append((loads, casts, stores))

    def colblock_unit(srcs, col0, blks, gidx=None):
        def loads(sv):
            out = []
            k0 = 0
            for (src, nk) in srcs:
                v = sv.rearrange("p (k c) -> p k c", k=8)[:, k0:k0 + nk, :]
                out.append((v, src[:, col0:col0 + 256].rearrange("(k p) c -> p k c", p=128)))
                k0 += nk
            return out

        def casts(sv, dv):
            g_ = None
            if gidx is not None:
                g_ = gT[:, gidx, :].unsqueeze(1).unsqueeze(3).to_broadcast([128, 2, 8, 128])
            return [(dv.rearrange("p (m k j) -> p m k j", m=2, k=8),
                     sv.rearrange("p (k m j) -> p m k j", k=8, m=2), g_)]
        add_unit(loads, casts, blks)

    def rowblock_unit(src, r0, blks):
        def loads(sv):
            return [(sv.rearrange("p (k c) -> p k c", k=2), src[r0:r0 + 256, :].rearrange("(k p) c -> p k c", p=128))]

        def casts(sv, dv):
            return [(dv, sv, None)]
        add_unit(loads, casts, blks)

    def kpair_unit(src, k0, c0, wg, blks, gidx=None):
        def loads(sv):
            return [(sv.rearrange("p (k c) -> p k c", k=4)[:, :, 0:wg],
                     src[k0 * 128:(k0 + 4) * 128, c0:c0 + wg].rearrange("(k p) c -> p k c", p=128))]

        def casts(sv, dv):
            g_ = None
            if gidx is not None:
                g_ = gT[:, gidx, k0:k0 + 4].unsqueeze(2).to_broadcast([128, 4, wg])
            return [(dv.rearrange("p (k c) -> p k c", k=4)[:, :, 0:wg], sv.rearrange("p (k c) -> p k c", k=4)[:, :, 0:wg], g_)]
        add_unit(loads, casts, blks)

    def ff_units(ff):
        for c0 in range(0, NCH, 2):
            gi_ = 0 if ff == "ff1" else 2
            colblock_unit([(W[ff, "gate"], 8)], c0 * 128, [BLK_ID[(ff, "gate", c0)], BLK_ID[(ff, "gate", c0 + 1)]], gidx=gi_)
            colblock_unit([(W[ff, "up"], 8)], c0 * 128, [BLK_ID[(ff, "up", c0)], BLK_ID[(ff, "up", c0 + 1)]], gidx=gi_)
            rowblock_unit(W[ff, "down"], c0 * 128, [BLK_ID[(ff, "down", c0)], BLK_ID[(ff, "down", c0 + 1)]])

    for kk2 in range(2):
        kpair_unit(w_mem, kk2 * 4, 0, 512, [BLK_ID[("wmem", kk2 * 2)], BLK_ID[("wmem", kk2 * 2 + 1)]], gidx=3)
    ff_units("ff1")
    n_prologue_units = len(conv_units)
    WIN_G = [(0, 512), (512, 512), (1024, 384), (1408, 384)]
    for g, (c0, wg) in enumerate(WIN_G):
        for kk2 in range(2):
            kpair_unit(w_in, kk2 * 4, c0, wg, [BLK_ID[("win", g, kk2 * 2)], BLK_ID[("win", g, kk2 * 2 + 1)]], gidx=1)
    for m0 in range(0, 8, 2):
        for j in range(3):
            colblock_unit([(w_gate, 8)], (j * 8 + m0) * 128, [BLK_ID[("wgate", j, m0)], BLK_ID[("wgate", j, m0 + 1)]], gidx=1)
        colblock_unit([(w_bra, 4), (w_brb, 2), (w_brc, 2)], m0 * 128, [BLK_ID[("br", m0)], BLK_ID[("br", m0 + 1)]])
    for k0 in range(0, 8, 2):
        rowblock_unit(w_out, k0 * 128, [BLK_ID[("wout", k0)], BLK_ID[("wout", k0 + 1)]])
    ff_units("ff2")

    def conv_step(n):
        for _ in range(n):
            i = conv_state["i"]
            if i >= len(conv_units):
                return
            conv_state["i"] = i + 1
            par = i % 2
            loads, casts, stores = conv_units[i]
            sv = stg32[:, par, :]
            dv = stg16[:, par, :]
            lds = loads(sv)
            P.op("sp", (lambda lds=lds: (lambda e: [e.dma_start(out=dst, in_=src) for (dst, src) in lds]))(),
                 writes=(("stg32", par),), dma=("stg32", par), ndma=len(lds))
            engines = ("dve", "act")
            ceng = engines[i % len(engines)]
            for (o_, i_, g_) in casts(sv, dv):
                if g_ is not None:
                    P.op("dve", (lambda o_=o_, i_=i_, g_=g_: (lambda e: e.tensor_tensor(out=o_, in0=i_, in1=g_, op=ALU.mult)))(),
                         reads=(("stg32", par), "c_gT"), writes=(("stg16", par),))
                    continue
                if ceng == "act":
                    fn = (lambda o_=o_, i_=i_: (lambda e: e.activation(out=o_, in_=i_, func=AF.Copy)))()
                else:
                    fn = (lambda o_=o_, i_=i_: (lambda e: e.tensor_copy(out=o_, in_=i_)))()
                P.op(ceng, fn, reads=(("stg32", par),), writes=(("stg16", par),))
            P.op("pool", (lambda stores=stores, dv=dv: (lambda e: [e.dma_start(out=wscr[:, blk, :], in_=dv[:, bi * 1024:(bi + 1) * 1024])
                                                                   for bi, blk in enumerate(stores)]))(),
                 reads=(("stg16", par),), writes=tuple(("scr", blk) for blk in stores), dma=("stg16", par), ndma=len(stores))

    stream = []
    for seq in range(2):
        stream += MEM_BLOCKS
        for t in range(NT):
            stream += TILE_BLOCKS
    if do_sample:
        stream += TILE_BLOCKS
    ws = {"loaded": 0, "pos": 0, "rel": 0}

    def ws_record_loads():
        while ws["loaded"] < len(stream) and ws["loaded"] < ws["rel"] + NSLOTS:
            i = ws["loaded"]
            blk = BLK_ID[stream[i]]
            sl = i % NSLOTS
            for la in range(min(i + CONV_LOOKAHEAD, len(stream) - 1), i - 1, -1):
                if ("scr", BLK_ID[stream[la]]) not in P.last_writer:
                    while ("scr", BLK_ID[stream[la]]) not in P.last_writer:
                        assert conv_state["i"] < len(conv_units)
                        conv_step(1)
                    break
            P.op("sp", (lambda blk=blk, sl=sl: (lambda e: e.dma_start(out=slots[:, sl, :], in_=wscr[:, blk, :])))(),
                 reads=(("scr", blk),), writes=(("slot", sl),), dma=("slot", sl))
            ws["loaded"] += 1

    def ws_get(expect):
        i = ws["pos"]
        assert stream[i] == expect, (stream[i], expect)
        assert i < ws["loaded"], "weight stream deadlock: increase NSLOTS"
        ws["pos"] += 1
        return i % NSLOTS

    def ws_done(n):
        ws["rel"] += n
        ws_record_loads()

    bank_rr = {"ffgu": 0, "ffd": 0, "tr": 0, "wout": 0}

    def norm_A(xsrc, xkey, s, so=0):
        ss = stat[:, so + s:so + s + 1]
        sq_ = stat[:, so + 8 + s:so + 9 + s]
        P.op("act", lambda e: e.activation(out=junk[:], in_=xsrc, func=AF.Square, accum_out=ss),
             reads=(xkey,), writes=(("ss", so, s),))
        P.op("act", lambda e: e.activation(out=sq_, in_=ss, func=AF.Sqrt, bias=epsc[:, 0:1], scale=1.0 / D),
             reads=(("ss", so, s), "c_eps"), writes=(("sqr", so, s),))

    def norm_B(xsrc, xkey, gidx, s, col0):
        norm_B1(xsrc, xkey, s)
        norm_B2(gidx, s, col0)

    def norm_B1(xsrc, xkey, s, so=0):
        par = s % 2
        sq_ = stat[:, so + 8 + s:so + 9 + s]
        rstd = stat[:, so + 16 + s:so + 17 + s]
        P.op("dve", lambda e: e.reciprocal(out=rstd, in_=sq_), reads=(("sqr", so, s),), writes=(("rstd", so, s),))
        P.op("dve", lambda e: e.tensor_scalar(out=xsb[:, par, :], in0=xsrc, scalar1=rstd, scalar2=None, op0=ALU.mult),
             reads=(xkey, ("rstd", so, s)), writes=(("xsb", par),))

    def norm_B2(gidx, s, col0, bank=None):
        par = s % 2
        if bank is None:
            b = 6 if bank_rr["tr"] % 2 == 0 else 4
            bank_rr["tr"] += 1
        else:
            b = bank

        def tr(e):
            out = []
            for k in range(8):
                out.append(e.transpose(out=psb(b + k // 4)[:, (k % 4) * 128:(k % 4 + 1) * 128], in_=xsb[:, par, k * 128:(k + 1) * 128],
                                       identity=identb16[:]))
            return out
        P.op("pe", tr, reads=(("xsb", par), "c_identb16"), writes=(("ps", b), ("ps", b + 1)))
        sub = col0 // 128
        P.op("act", lambda e: e.activation(out=hT[:, 0:4, col0:col0 + 128], in_=psb(b)[:, 0:512].rearrange("p (k c) -> p k c", k=4),
                                           func=AF.Copy),
             reads=(("ps", b),), writes=tuple(("hT", k, sub) for k in range(4)))
        P.op("dve", lambda e: e.tensor_copy(out=hT[:, 4:8, col0:col0 + 128], in_=psb(b + 1)[:, 0:512].rearrange("p (k c) -> p k c", k=4)),
             reads=(("ps", b + 1),), writes=tuple(("hT", k, sub) for k in range(4, 8)))

    def norm_to_hT(xsrc, xkey, gidx, s, col0):
        norm_A(xsrc, xkey, s)
        norm_B(xsrc, xkey, gidx, s, col0)

    class NormPipe:
        def __init__(self, gidx, alt=None, so=0, bank=None):
            self.gidx = gidx
            self.pending = None
            self.pending2 = None
            self.alt = XCUR["alt"] if alt is None else alt
            self.so = so
            self.bank = bank

        def feed(self, s):
            xa_, xk_ = XB(s, self.alt)
            norm_A(xa_, xk_, s, self.so)
            if self.pending is not None:
                p_ = self.pending
                xa_, xk_ = XB(p_, self.alt)
                norm_B1(xa_, xk_, p_, self.so)
            if self.pending2 is not None:
                norm_B2(self.gidx, self.pending2, self.pending2 * 128, self.bank)
            self.pending2 = self.pending
            self.pending = s

        def flush_step1(self):
            if self.pending is not None:
                p_ = self.pending
                xa_, xk_ = XB(p_, self.alt)
                norm_B1(xa_, xk_, p_, self.so)
            if self.pending2 is not None:
                norm_B2(self.gidx, self.pending2, self.pending2 * 128, self.bank)
            self.pending2 = None

        def flush_step2(self):
            if self.pending is not None:
                norm_B2(self.gidx, self.pending, self.pending * 128, self.bank)
            self.pending = None

        def flush(self):
            self.flush_step1()
            self.flush_step2()

    class FinalPipe:
        def __init__(self, out_rows):
            self.out_rows = out_rows
            self.pending = None

        def feed(self, s):
            final_A(s)
            if self.pending is not None:
                final_B(self.pending, self.out_rows)
            self.pending = s

        def flush(self):
            if self.pending is not None:
                final_B(self.pending, self.out_rows)
                self.pending = None

    def hT_keys(ns):
        return tuple(("hT", k, s) for k in range(8) for s in range(ns))

    def big_keys(c, ns):
        return tuple(("big", c, s) for s in range(ns))

    def ffn(ff, NS, post_sub=None):
        TT = NS * 128
        for half in (HALF_A, HALF_B):
            for ci, c in enumerate(half):
                sg = ws_get((ff, "gate", c))
                su = ws_get((ff, "up", c))
                bg = (bank_rr["ffgu"] % 2) * 2
                bu = bg + 1
                bank_rr["ffgu"] += 1
                par = ci % 2

                def mm(e, sl, bk):
                    out = []
                    for k in range(8):
                        out.append(e.matmul(ps[:, bk, 0:TT], lhsT=slots[:, sl, k * 128:(k + 1) * 128], rhs=hT[:, k, 0:TT],
                                            start=(k == 0), stop=(k == 7)))
                    return out
                P.op("pe", (lambda sg=sg, bg=bg: (lambda e: mm(e, sg, bg)))(), reads=(("slot", sg),) + hT_keys(NS), writes=(("ps", bg),))
                P.op("pe", (lambda su=su, bu=bu: (lambda e: mm(e, su, bu)))(), reads=(("slot", su),) + hT_keys(NS), writes=(("ps", bu),))
                ws_done(2)
                P.op("act", (lambda bg=bg, par=par: (lambda e: e.activation(out=sgt[:, par, 0:TT], in_=ps[:, bg, 0:TT], func=AF.Silu)))(),
                     reads=(("ps", bg),), writes=(("sig", par),))
                P.op("dve", (lambda bu=bu, par=par, ci=ci: (lambda e: e.tensor_tensor(out=big[:, ci, 0:TT], in0=ps[:, bu, 0:TT],
                                                                                     in1=sgt[:, par, 0:TT], op=ALU.mult)))(),
                     reads=(("ps", bu), ("sig", par)), writes=big_keys(ci, NS))
            dsl = [ws_get((ff, "down", c)) for c in half]
            for s in range(NS):
                for n in range(2):
                    b = 4 + bank_rr["ffd"] % 2
                    bank_rr["ffd"] += 1

                    def mmd(e, s=s, n=n, b=b, dsl=dsl):
                        out = []
                        for ci in range(11):
                            out.append(e.matmul(ps[:, b, :], lhsT=big[:, ci, s * 128:(s + 1) * 128],
                                                rhs=slots[:, dsl[ci], n * 512:(n + 1) * 512], start=(ci == 0), stop=(ci == 10)))
                        return out
                    P.op("pe", mmd, reads=tuple(("slot", q) for q in dsl) + tuple(("big", ci, s) for ci in range(11)),
                         writes=(("ps", b),))
                    xa_, xk_ = XB(s)
                    P.op("dve", (lambda xa_=xa_, n=n, b=b: (lambda e: e.scalar_tensor_tensor(
                        out=xa_[:, n * 512:(n + 1) * 512], in0=ps[:, b, :], scalar=0.5, in1=xa_[:, n * 512:(n + 1) * 512],
                        op0=ALU.mult, op1=ALU.add)))(), reads=(("ps", b), xk_), writes=(xk_,))
                if half is HALF_B and post_sub is not None:
                    post_sub(s)
            ws_done(11)

    def headnorm(psrc_list, nheads, gtab, bp=0):
        W_ = nheads * 64
        qn_, scr_ = QN[bp], SCR[bp]
        qk, sk, ssk = ("qn", bp), ("scr6", bp), ("ssq", bp)
        sc0 = 24 + 24 * bp
        for (pap, c0, w, bk) in psrc_list:
            P.op("act", (lambda pap=pap, c0=c0, w=w: (lambda e: e.activation(out=scr_[:, c0:c0 + w], in_=pap, func=AF.Square)))(),
                 reads=(("ps", bk),), writes=(sk,), grp=("hnsq", bp))
        ssq = stat[:, sc0:sc0 + nheads]
        P.op("dve", lambda e: e.tensor_reduce(out=ssq, in_=scr_[:, 0:W_].rearrange("p (h d) -> p h d", d=64), axis=AX.X, op=ALU.add),
             reads=(sk,), writes=(ssk,))
        P.op("act", lambda e: e.activation(out=ssq, in_=ssq, func=AF.Sqrt, bias=epsc[:, 0:1], scale=1.0 / 64),
             reads=(ssk, "c_eps"), writes=(ssk,))
        P.op("dve", lambda e: e.reciprocal(out=ssq, in_=ssq), reads=(ssk,), writes=(ssk,))
        for (pap, c0, w, bk) in psrc_list:
            nh = w // 64
            h0 = c0 // 64
            P.op("dve", (lambda pap=pap, c0=c0, w=w, nh=nh, h0=h0: (lambda e: e.tensor_tensor(
                out=qn_[:, c0:c0 + w].rearrange("p (h d) -> p h d", d=64), in0=pap.rearrange("p (h d) -> p h d", d=64),
                in1=stat[:, sc0 + h0:sc0 + h0 + nh].unsqueeze(2).to_broadcast([128, nh, 64]), op=ALU.mult)))(),
                reads=(("ps", bk), ssk), writes=(qk,), grp=("hnp1", bp))
        P.op("pool", lambda e: e.tensor_tensor(out=qn_[:, 0:W_], in0=qn_[:, 0:W_], in1=gtab, op=ALU.mult),
             reads=(qk, "c_gqk", "c_gkc"), writes=(qk,))

    def vaug_write(dst_tile_ap, src_ap, nh_src, reads, wkey, eng_cycle=("act",)):
        d4 = dst_tile_ap.rearrange("p (a b c) -> p a b c", a=2, b=2)
        if nh_src == 2:
            s3 = src_ap.rearrange("p (k d) -> p k d", d=64)
            pairs = [(d4[:, :, 0, 0:64], s3), (d4[:, :, 1, 64:128], s3)]
        else:
            s4 = src_ap.rearrange("p (a b d) -> p a b d", a=2, b=2)
            pairs = [(d4[:, :, 0, 0:64], s4[:, :, 0, :]), (d4[:, :, 1, 64:128], s4[:, :, 1, :])]
        for i, (o_, i_) in enumerate(pairs):
            eng = eng_cycle[i % len(eng_cycle)]
            if eng == "act":
                fn = (lambda o_=o_, i_=i_: (lambda e: e.activation(out=o_, in_=i_, func=AF.Copy)))()
            else:
                fn = (lambda o_=o_, i_=i_: (lambda e: e.tensor_copy(out=o_, in_=i_)))()
            P.op(eng, fn, reads=reads, writes=(wkey,))

    def transposes_to(srcs, dests, reads, banks):
        def tr(e):
            out = []
            for t, src in enumerate(srcs):
                out.append(e.transpose(out=ps[:, banks[t // 4], (t % 4) * 128:(t % 4 + 1) * 128], in_=src, identity=identb[:]))
            return out
        nb = (len(srcs) + 3) // 4
        P.op("pe", tr, reads=reads + ("c_identb",), writes=tuple(("ps", banks[i]) for i in range(nb)))
        for i, (t0, n, oap, wkeys) in enumerate(dests):
            assert t0 // 4 == (t0 + n - 1) // 4
            bk = banks[t0 // 4]
            iap = ps[:, bk, (t0 % 4) * 128:(t0 % 4 + n) * 128].rearrange("p (n c) -> p n c", n=n)
            eng = "act" if i % 2 == 0 else "dve"
            if eng == "act":
                fn = (lambda oap=oap, iap=iap: (lambda e: e.activation(out=oap, in_=iap, func=AF.Copy)))()
            else:
                fn = (lambda oap=oap, iap=iap: (lambda e: e.tensor_copy(out=oap, in_=iap)))()
            P.op(eng, fn, reads=(("ps", bk),), writes=wkeys)

    def keylist(slots_j):
        by_t = {}
        for (sl, j) in slots_j:
            t = sl // 2
            ent = by_t.setdefault(t, [False, False, None, None])
            ent[sl % 2] = True
            ent[2 + sl % 2] = j
        return [(t % 8, v[0], v[1], v[2], v[3]) for t, v in sorted(by_t.items())]

    def attend_chunk(qc0, klA, klB):
        ycols = slice(qc0, qc0 + 64)
        sidx = qc0 // 128
        nA = len(klA)
        nB = len(klB)

        class _St:
            pass
        stg = _St()

        def sa(e):
            out = []
            for i, (t, lo, hi, _, _) in enumerate(klA):
                for kvg in range(2):
                    for par_ in range(2):
                        hp = par_ * 64
                        pp0 = kvg * 2
                        c_ = i * 256 + pp0 * 64
                        out.append(e.matmul(ps[:, par_, c_:c_ + 128].rearrange("p (a q) -> p a q", a=2),
                                            lhsT=kaT[hp:hp + 64, kvg, t * 128:(t + 1) * 128],
                                            rhs=big[hp:hp + 64, pp0:pp0 + 2, ycols], start=True, stop=True))
            return out
        def stage_sa():
            P.op("pe", sa, reads=tuple(("kaT", t) for (t, _, _, _, _) in klA) + tuple(("big", c, sidx) for c in range(4)),
                 writes=(("ps", 0), ("ps", 1)))
            P.op("act", lambda e: e.activation(out=pA[:, :, 0:nA * 256], in_=ps[:, 0:2, 0:nA * 256], func=AF.Exp, scale=0.125),
                 reads=(("ps", 0), ("ps", 1)), writes=("pA",))
        stg.sa = stage_sa

        def sbm(e):
            out = []
            for i, (t, lo, hi, jlo, jhi) in enumerate(klB):
                bl = lo and jlo is not None and jlo <= 2
                bh = hi and jhi is not None and jhi <= 2
                for par_ in range(2):
                    hp = par_ * 64
                    c_ = i * 128
                    bk = 2 + par_ * 2 + c_ // 512
                    reg = ps[:, bk, c_ % 512:c_ % 512 + 128]
                    started = False
                    if bl:
                        out.append(e.matmul(reg.rearrange("p (a q) -> p a q", a=2), lhsT=zmatb[hp:hp + 64, 64:192],
                                            rhs=biasb[hp:hp + 64, par_:4:2, jlo, :], start=True, stop=False))
                        started = True
                    if bh:
                        out.append(e.matmul(reg.rearrange("p (a q) -> p a q", a=2), lhsT=zmatb[hp:hp + 64, 0:128],
                                            rhs=biasb[hp:hp + 64, par_:4:2, jhi, :], start=(not started), stop=False))
                        started = True
                    for hh in range(2):
                        out.append(e.matmul(reg[:, hh * 64:(hh + 1) * 64], lhsT=kbT[hp:hp + 64, hh, t * 128:(t + 1) * 128],
                                            rhs=big[hp:hp + 64, 4 + hh, ycols], start=(not started), stop=True))
            return out
        bankB = (("ps", 2), ("ps", 3), ("ps", 4), ("ps", 5))
        psB = ps[:, 2:6, :].rearrange("p (par b) c -> p par (b c)", par=2)

        def stage_sb():
            P.op("pe", sbm, reads=tuple(("kbT", t) for (t, _, _, _, _) in klB) + tuple(("big", 4 + c, sidx) for c in range(2))
                 + ("c_zmatb", "c_biasb"), writes=bankB)
            P.op("act", lambda e: e.activation(out=pB[:, :, 0:nB * 128], in_=psB[:, :, 0:nB * 128], func=AF.Exp, scale=0.125),
                 reads=bankB, writes=("pB",))
        stg.sb = stage_sb

        def prange(lo, hi):
            if lo and hi:
                return 0, 128
            return (0, 64) if lo else (64, 128)

        def pva(e):
            out = []
            for kvg in range(2):
                for par_ in range(2):
                    blk = kvg * 2 + par_
                    pp0 = kvg * 2
                    for i, (t, lo, hi, _, _) in enumerate(klA):
                        p0, p1 = prange(lo, hi)
                        c_ = i * 256 + pp0 * 64
                        oc = par_ * 256 + pp0 * 64
                        out.append(e.matmul(ps[:, 6, oc:oc + 128], lhsT=vA[p0:p1, t, blk * 128:(blk + 1) * 128],
                                            rhs=pA[p0:p1, par_, c_:c_ + 128], start=(i == 0), stop=(i == nA - 1)))
            return out

        def pvb(e):
            out = []
            for h in range(4):
                for i, (t, lo, hi, _, _) in enumerate(klB):
                    p0, p1 = prange(lo, hi)
                    c_ = i * 128 + (h // 2) * 64
                    out.append(e.matmul(ps[:, 7, h * 64:(h + 1) * 64], lhsT=vB[p0:p1, t, h * 128:(h + 1) * 128],
                                        rhs=pB[p0:p1, h % 2, c_:c_ + 64], start=(i == 0), stop=(i == nB - 1)))
            return out
        oA = ps[:, 6, :].rearrange("p (two h q) -> p h two q", two=2, q=64)
        sx = sinkexp[:].rearrange("p (h two) -> p h two", two=2)
        denA = den[:, 0:512].rearrange("p (h two q) -> p h two q", two=2, q=64)
        recA = rec[:, 0:512].rearrange("p (h two q) -> p h two q", two=2, q=64)
        def stage_pva():
          P.op("pe", pva, reads=("pA",) + tuple(("vA", t) for (t, _, _, _, _) in klA), writes=(("ps", 6),))
          for par, (np0, dp0) in enumerate(((0, 64), (64, 0))):
            P.op("dve", (lambda par=par, dp0=dp0: (lambda e: e.tensor_tensor(
                out=denA[dp0:dp0 + 64, :, par, :], in0=oA[dp0:dp0 + 64, :, par, :],
                in1=sx[dp0:dp0 + 64, :, par].unsqueeze(2).to_broadcast([64, 4, 64]), op=ALU.add)))(),
                reads=(("ps", 6), "c_sink"), writes=(("den", par),))
            P.op("dve", (lambda par=par, dp0=dp0, np0=np0: (lambda e: e.reciprocal(
                out=recA[np0:np0 + 64, :, par, :], in_=denA[dp0:dp0 + 64, :, par, :])))(),
                reads=(("den", par),), writes=(("rec", par),))
            P.op("dve", (lambda par=par, np0=np0: (lambda e: e.tensor_tensor(
                out=big[np0:np0 + 64, 8:12, ycols], in0=oA[np0:np0 + 64, :, par, :], in1=recA[np0:np0 + 64, :, par, :],
                op=ALU.mult)))(),
                reads=(("ps", 6), ("rec", par)), writes=tuple(("big", 8 + c, sidx) for c in range(4)), grp="yT")
        stg.pva = stage_pva
        oB = ps[:, 7, 0:256].rearrange("p (h two q) -> p h two q", two=2, q=64)
        recB = rec[:, 512:768].rearrange("p (h two q) -> p h two q", two=2, q=64)

        oBs = vf32[:, 0:256].rearrange("p (h two q) -> p h two q", two=2, q=64)

        def stage_pvb():
          P.op("pe", pvb, reads=("pB",) + tuple(("vB", t) for (t, _, _, _, _) in klB), writes=(("ps", 7),))
          P.op("act", lambda e: e.activation(out=vf32[:, 0:256], in_=ps[:, 7, 0:256], func=AF.Copy), reads=(("ps", 7),), writes=("vf32",))
          for par, (np0, dp0) in enumerate(((0, 64), (64, 0))):
            P.op("dve", (lambda par=par, dp0=dp0, np0=np0: (lambda e: e.reciprocal(
                out=recB[np0:np0 + 64, :, par, :], in_=oBs[dp0:dp0 + 64, :, par, :])))(),
                reads=("vf32",), writes=(("recB", par),))
            P.op("pool", (lambda par=par, np0=np0: (lambda e: e.tensor_tensor(
                out=big[np0:np0 + 64, 12:14, ycols], in0=oBs[np0:np0 + 64, :, par, :], in1=recB[np0:np0 + 64, :, par, :],
                op=ALU.mult)))(),
                reads=("vf32", ("recB", par)), writes=tuple(("big", 12 + c, sidx) for c in range(2)), grp="yTB")
        stg.pvb = stage_pvb
        return stg

    def attend_mem(c0, n, mi):
        sidxs = sorted(set(range(c0 // 128, (c0 + n + 127) // 128)))
        pCs = [(pB[:, :, 0:512], "pB"), (pA[:, :, :], "pA")]

        def scores(h):
            hp = (h % 2) * 64
            b0 = (h % 2) * 2
            pC, pkey = pCs[h % 2]

            def sc(e):
                out = []
                for t in range(2):
                    out.append(e.matmul(ps[:, b0 + t, 0:n], lhsT=mkT[hp:hp + 64, mi, h // 2, t * 128:(t + 1) * 128],
                                        rhs=big[hp:hp + 64, 6 + h // 2, c0:c0 + n], start=True, stop=True))
                return out
            P.op("pe", sc, reads=(("mkT", mi),) + tuple(("big", 6 + h // 2, s) for s in sidxs), writes=(("ps", b0), ("ps", b0 + 1)))
            P.op("act", lambda e: e.activation(out=pC[:, :, 0:n], in_=ps[:, b0:b0 + 2, 0:n], func=AF.Exp, scale=0.125),
                 reads=(("ps", b0), ("ps", b0 + 1)), writes=(pkey,))

        def pvn(h):
            pC, pkey = pCs[h % 2]
            bo = 6 + h % 2
            rk = ("recC", h % 2)
            rbuf = rec[:, 0:512] if h % 2 == 0 else den[:, 0:512]

            def pv(e):
                out = []
                for t in range(2):
                    out.append(e.matmul(ps[:, bo, 0:n], lhsT=vM[:, mi, t, h * 128:(h + 1) * 128], rhs=pC[:, t, 0:n],
                                        start=(t == 0), stop=(t == 1)))
                return out
            P.op("pe", pv, reads=(pkey, ("vM", mi)), writes=(("ps", bo),))
            np0, dp0 = ((0, 64), (64, 0))[h % 2]
            P.op("dve", lambda e: e.reciprocal(out=rbuf[np0:np0 + 64, 0:n], in_=ps[dp0:dp0 + 64, bo, 0:n]),
                 reads=(("ps", bo), ("rec", 0), ("rec", 1), ("den", 0), ("den", 1)), writes=(rk, ("rec", 0), ("rec", 1), ("den", 0), ("den", 1)))
            P.op("dve", lambda e: e.tensor_tensor(out=big[np0:np0 + 64, 14 + h // 2, c0:c0 + n],
                                                  in0=ps[np0:np0 + 64, bo, 0:n], in1=rbuf[np0:np0 + 64, 0:n], op=ALU.mult),
                 reads=(("ps", bo), rk), writes=tuple(("big", 14 + h // 2, s) for s in sidxs), grp="yT")

        scores(0)
        for h in range(1, 4):
            scores(h)
            pvn(h - 1)
        pvn(3)

    GTAB_QK = gqk[:, 0:1408]

    def mixer(NS, tile_info):
        TT = NS * 128
        kind = tile_info["kind"]
        rpar = tile_info["rpar"]
        if kind == "prompt":
            ti = tile_info["ti"]
            P.op("pool", lambda e: e.dma_start(out=rope[:, rpar, :, :], in_=ropeP_in[ti * 512:(ti + 1) * 512, :].rearrange("(s p) c -> p s c", p=128)),
                 writes=(("rope", rpar),), dma=("rope", rpar))
        else:
            P.op("pool", lambda e: e.dma_start(out=rope[:, rpar, 0, :], in_=ropeS_in), writes=(("rope", rpar),), dma=("rope", rpar))
        win_sl = {}
        for g in range(4):
            for kk in range(4):
                win_sl[g, kk] = ws_get(("win", g, kk))
        dbl = tile_info.get("dbl", False)

        def chain(s):
            bset = 4 * (s % 2)
            bp = (s % 2) if dbl else 0
            qn_, scr_, kad_ = QN[bp], SCR[bp], KAD[bp]
            qk, sk = ("qn", bp), ("scr6", bp)

            def proj(e):
                out = []
                for k in range(8):
                    for g, (c0, wg) in enumerate(WIN_G):
                        out.append(e.matmul(ps[:, bset + g, 0:wg], lhsT=hT[:, k, s * 128:(s + 1) * 128],
                                            rhs=slots[:, win_sl[g, k // 2], (k % 2) * 512:(k % 2) * 512 + wg],
                                            start=(k == 0), stop=(k == 7)))
                return out
            P.op("pe", proj, reads=tuple(("slot", v) for v in win_sl.values()) + tuple(("hT", k, s) for k in range(8)),
                 writes=tuple(("ps", bset + g) for g in range(4)))
            if s == NS - 1:
                ws_done(16)
            headnorm([(ps[:, bset + 0, :], 0, 512, bset + 0), (ps[:, bset + 1, :], 512, 512, bset + 1),
                      (ps[:, bset + 2, 0:384], 1024, 384, bset + 2)], 22, GTAB_QK, bp)
            q4 = qn_[:, 0:640].rearrange("p (h two d) -> p h two d", two=2, d=32)
            x1 = q4[:, :, 0, :]
            x2 = q4[:, :, 1, :]
            rs_ = s if kind == "prompt" else 0
            cosb = rope[:, rpar, rs_, 0:32].unsqueeze(1).to_broadcast([128, 10, 32])
            sinb = rope[:, rpar, rs_, 32:64].unsqueeze(1).to_broadcast([128, 10, 32])
            tmp = scr_[:, 0:1280].rearrange("p (a h d) -> p a h d", a=4, d=32)

            def tt(o_, a_, b_, op_):
                return lambda e: e.tensor_tensor(out=o_, in0=a_, in1=b_, op=op_)
            gname = ("ropetmp", bp)
            P.op("pool", tt(tmp[:, 0], x1, cosb, ALU.mult), reads=(qk, ("rope", rpar)), writes=(sk,), grp=gname)
            P.op("pool", tt(tmp[:, 1], x2, sinb, ALU.mult), reads=(qk, ("rope", rpar)), writes=(sk,), grp=gname)
            P.op("pool", tt(tmp[:, 2], x1, sinb, ALU.mult), reads=(qk, ("rope", rpar)), writes=(sk,), grp=gname)
            P.op("pool", tt(tmp[:, 3], x2, cosb, ALU.mult), reads=(qk, ("rope", rpar)), writes=(sk,), grp=gname)
            P.op("pool", tt(x1, tmp[:, 0], tmp[:, 1], ALU.subtract), reads=(sk,), writes=(qk,))
            P.op("pool", tt(x2, tmp[:, 2], tmp[:, 3], ALU.add), reads=(sk,), writes=(qk,))
            if kind == "prompt":
                seq, ti = tile_info["seq"], tile_info["ti"]
                gt = (ti * 4 + s) % 8
                last = (ti == NT - 1)
            else:
                seq = None
                gt = 0
                last = True
            if last:
                P.op("act", lambda e: e.activation(out=vf32[:], in_=ps[:, bset + 3, 0:384], func=AF.Copy),
                     reads=(("ps", bset + 3),), writes=("vf32",))
            vaug_write(vA[:, gt, :], ps[:, bset + 3, 0:128], 2, (("ps", bset + 3),), ("vA", gt))
            vaug_write(vB[:, gt, :], ps[:, bset + 3, 128:384], 4, (("ps", bset + 3),), ("vB", gt))
            if last:
                dq = ("qn", bp)
                if kind == "prompt":
                    r0 = s * 128
                    P.op("pool", lambda e: e.dma_start(out=bkp[seq, r0:r0 + 128, :], in_=qn_[:, 896:1152]),
                         reads=(qk,), writes=(("o_bkp", seq, s),), dma=dq)
                    P.op("pool", lambda e: e.dma_start(out=bvp[seq, r0:r0 + 128, :], in_=vf32[:, 128:384]),
                         reads=("vf32",), writes=(("o_bvp", seq, s),), dma="vf32")
                    if s == NS - 1:
                        P.op("pool", lambda e: e.dma_start(out=akp[seq, :, :], in_=qn_[:, 512:640]),
                             reads=(qk,), writes=(("o_akp", seq),), dma=dq)
                        P.op("pool", lambda e: e.dma_start(out=avp[seq, :, :], in_=vf32[:, 0:128]),
                             reads=("vf32",), writes=(("o_avp", seq),), dma="vf32")
                else:
                    P.op("pool", lambda e: e.dma_start(out=bks, in_=qn_[:, 896:1152]), reads=(qk,), writes=("o_bks",), dma=dq)
                    P.op("pool", lambda e: e.dma_start(out=bvs, in_=vf32[:, 128:384]), reads=("vf32",), writes=("o_bvs",), dma="vf32")
                    P.op("pool", lambda e: e.dma_start(out=aks, in_=qn_[:, 512:640]), reads=(qk,), writes=("o_aks",), dma=dq)
                    P.op("pool", lambda e: e.dma_start(out=avs, in_=vf32[:, 0:128]), reads=("vf32",), writes=("o_avs",), dma="vf32")
            P.op("pool", lambda e: e.tensor_copy(out=kad_.rearrange("p (k two d) -> p k two d", two=2, d=64),
                                                 in_=qn_[:, 512:640].rearrange("p (k d) -> p k d", d=64).unsqueeze(2).to_broadcast([128, 2, 2, 64])),
                 reads=(qk,), writes=(("kadup", bp),))

        def trans(s):
            bset = 4 * (s % 2)
            bp = (s % 2) if dbl else 0
            qn_, kad_ = QN[bp], KAD[bp]
            if kind == "prompt":
                gt = (tile_info["ti"] * 4 + s) % 8
            else:
                gt = 0
            rc = gt * 128
            srcs = [qn_[:, t * 128:(t + 1) * 128] for t in range(4)] + [kad_[:, 0:128], kad_[:, 128:256]] + \
                   [qn_[:, 640 + t * 128:640 + (t + 1) * 128] for t in range(6)]
            transposes_to(srcs, [
                (0, 4, big[:, 0:4, s * 128:(s + 1) * 128], tuple(("big", c, s) for c in range(4))),
                (4, 2, kaT[:, :, rc:rc + 128], (("kaT", gt),)),
                (6, 2, big[:, 4:6, s * 128:(s + 1) * 128], tuple(("big", c, s) for c in (4, 5))),
                (8, 2, kbT[:, :, rc:rc + 128], (("kbT", gt),)),
                (10, 2, big[:, 6:8, s * 128:(s + 1) * 128], tuple(("big", c, s) for c in (6, 7))),
            ], (("qn", bp), ("kadup", bp)), [bset + 0, bset + 1, bset + 2])

        att = {"prev": None}

        def attn(cl):
            c = tile_info["ti"] * 8 + cl
            klA = keylist([(kc, c - kc) for kc in range(max(0, c - 2), c + 1)])
            klB = keylist([(kc, c - kc) for kc in range(max(0, c - 8), c + 1)])
            stg_ = attend_chunk(cl * 64, klA, klB)
            stg_.sa()
            if att["prev"] is not None:
                att["prev"].pvb()
            stg_.sb()
            stg_.pva()
            att["prev"] = stg_

        npipe = tile_info.get("npipe")
        if dbl and kind == "prompt":
            chain(0)
            chain(1)
            npipe.flush_step1()
            trans(0)
            attn(0); attn(1)
            chain(2)
            npipe.flush_step2()
            trans(1)
            attn(2); attn(3)
            chain(3)
            trans(2)
            attn(4); attn(5)
            trans(3)
            attn(6); attn(7)
            att["prev"].pvb()
            attend_mem(0, 512, 0)
        elif kind == "prompt":
            npipe.flush()
            for s in range(NS):
                chain(s)
                trans(s)
            for cl in range(8):
                attn(cl)
            att["prev"].pvb()
            attend_mem(0, 512, 0)
        else:
            for s in range(NS):
                chain(s)
                trans(s)
        if False:
            pass
        else:
            pass

    def post_attention(NS, post_sub=None):
        TT = NS * 128
        for m in range(8):
            gs = [ws_get(("wgate", j, m)) for j in range(3)]
            bs_ = ws_get(("br", m))
            for j in range(3):
                def mmg(e, j=j, sl=gs[j]):
                    out = []
                    for k in range(8):
                        out.append(e.matmul(ps[:, j, 0:TT], lhsT=slots[:, sl, k * 128:(k + 1) * 128], rhs=hT[:, k, 0:TT],
                                            start=(k == 0), stop=(k == 7)))
                    return out
                P.op("pe", mmg, reads=(("slot", gs[j]),) + hT_keys(NS), writes=(("ps", j),))
                P.op("act", (lambda j=j, m=m: (lambda e: e.activation(out=sig[:, j, 0:TT], in_=ps[:, j, 0:TT], func=AF.Sigmoid,
                                                                       bias=bgT[:, j * 8 + m:j * 8 + m + 1], scale=1.0)))(),
                     reads=(("ps", j), "c_bgT"), writes=(("sig", j),))
            ycs = [(8, 4, 0), (12, 2, 512), (14, 2, 768)]
            for j, (yc0, nk, off) in enumerate(ycs):
                def mmb(e, j=j, yc0=yc0, nk=nk, off=off, bs_=bs_):
                    out = []
                    for k in range(nk):
                        out.append(e.matmul(ps[:, 3 + j, 0:TT], lhsT=slots[:, bs_, off + k * 128:off + (k + 1) * 128],
                                            rhs=big[:, yc0 + k, 0:TT], start=(k == 0), stop=(k == nk - 1)))
                    return out
                rk = tuple(("big", yc0 + k, s) for k in range(nk) for s in range(NS))
                P.op("pe", mmb, reads=(("slot", bs_),) + rk, writes=(("ps", 3 + j),))
            ws_done(4)
            tj = scr6[:].rearrange("p (j c) -> p j c", j=3)
            for j in range(3):
                P.op("dve", (lambda j=j: (lambda e: e.tensor_tensor(out=tj[:, j, 0:TT], in0=ps[:, 3 + j, 0:TT], in1=sig[:, j, 0:TT], op=ALU.mult)))(),
                     reads=(("ps", 3 + j), ("sig", j)), writes=(("tj", j), ("scr6", 0)) if j == 0 else (("tj", j),))
            P.op("pool", lambda e: e.tensor_tensor(out=tj[:, 0, 0:TT], in0=tj[:, 0, 0:TT], in1=tj[:, 1, 0:TT], op=ALU.add),
                 reads=(("tj", 0), ("tj", 1)), writes=(("tj", 0),))
            P.op("pool", (lambda m=m: (lambda e: e.tensor_tensor(out=big[:, m, 0:TT], in0=tj[:, 0, 0:TT], in1=tj[:, 2, 0:TT], op=ALU.add)))(),
                 reads=(("tj", 0), ("tj", 2)), writes=big_keys(m, NS) + (("scr6", 0),))
        osl = [ws_get(("wout", k)) for k in range(8)]
        for s in range(NS):
            for n in range(2):
                b = 6 + bank_rr["wout"] % 2
                bank_rr["wout"] += 1

                def mmo(e, s=s, n=n, b=b):
                    out = []
                    for k in range(8):
                        out.append(e.matmul(ps[:, b, :], lhsT=big[:, k, s * 128:(s + 1) * 128], rhs=slots[:, osl[k], n * 512:(n + 1) * 512],
                                            start=(k == 0), stop=(k == 7)))
                    return out
                P.op("pe", mmo, reads=tuple(("slot", q) for q in osl) + tuple(("big", k, s) for k in range(8)), writes=(("ps", b),))
                xa_, xk_ = XB(s)
                P.op("dve", (lambda xa_=xa_, n=n, b=b: (lambda e: e.tensor_tensor(out=xa_[:, n * 512:(n + 1) * 512], in0=ps[:, b, :],
                                                                                 in1=xa_[:, n * 512:(n + 1) * 512], op=ALU.add)))(),
                     reads=(("ps", b), xk_), writes=(xk_,))
            if post_sub is not None:
                post_sub(s)
        ws_done(8)

    def final_A(s):
        ss = stat[:, s:s + 1]
        sq_ = stat[:, 8 + s:9 + s]
        xa_, xk_ = XB(s)
        P.op("act", lambda e: e.activation(out=junk[:], in_=xa_, func=AF.Square, accum_out=ss),
             reads=(xk_,), writes=(("ss", 0, s),))
        P.op("act", lambda e: e.activation(out=sq_, in_=ss, func=AF.Sqrt, bias=epsc[:, 0:1], scale=1.0 / D),
             reads=(("ss", 0, s), "c_eps"), writes=(("sqr", 0, s),))

    def final_B(s, out_rows):
        par = s % 2
        sq_ = stat[:, 8 + s:9 + s]
        rstd = stat[:, 16 + s:17 + s]
        xa_, xk_ = XB(s)
        P.op("dve", lambda e: e.reciprocal(out=rstd, in_=sq_), reads=(("sqr", 0, s),), writes=(("rstd", 0, s),))
        P.op("dve", lambda e: e.scalar_tensor_tensor(out=ystage[:, par, :], in0=xa_, scalar=rstd, in1=gfin[:],
                                                     op0=ALU.mult, op1=ALU.mult),
             reads=(xk_, ("rstd", 0, s), "c_gfin"), writes=(("ystage", par),))
        dst = out_rows(s)
        P.op("pool", lambda e: e.dma_start(out=dst, in_=ystage[:, par, :]),
             reads=(("ystage", par),), writes=(("o_y", id(dst)),), dma=("ystage", par))
        keep_alive.append(dst)

    def mem_phase(seq):
        msl = [ws_get(("wmem", kk)) for kk in range(4)]
        for j in range(2):
            P.op("pool", (lambda j=j: (lambda e: e.dma_start(out=ystage[:, j, :], in_=memp[seq, j * 128:(j + 1) * 128, :])))(),
                 writes=(("ystage", j),), dma=("ystage", j))
            chk(1.1)
            norm_to_hT(ystage[:, j, :], ("ystage", j), 3, j, j * 128)
            chk(1.2)
            b = 4 + j

            def mm(e, j=j, b=b):
                out = []
                for k in range(8):
                    out.append(e.matmul(ps[:, b, :], lhsT=hT[:, k, j * 128:(j + 1) * 128], rhs=slots[:, msl[k // 2], (k % 2) * 512:(k % 2) * 512 + 512],
                                        start=(k == 0), stop=(k == 7)))
                return out
            P.op("pe", mm, reads=tuple(("slot", q) for q in msl) + tuple(("hT", k, j) for k in range(8)), writes=(("ps", b),))
            chk(1.3)
            headnorm([(ps[:, b, 0:256], 0, 256, b)], 4, gkc[:], 0)
            chk(1.4)
            P.op("pool", (lambda j=j: (lambda e: e.dma_start(out=mkp[seq, j * 128:(j + 1) * 128, :], in_=qn[:, 0:256])))(),
                 reads=(("qn", 0),), writes=(("o_mkp", seq, j),), dma=("qn", 0))
            P.op("act", (lambda b=b: (lambda e: e.activation(out=vf32[:, 0:256], in_=ps[:, b, 256:512], func=AF.Copy)))(),
                 reads=(("ps", b),), writes=("vf32",))
            P.op("pool", (lambda j=j: (lambda e: e.dma_start(out=mvp[seq, j * 128:(j + 1) * 128, :], in_=vf32[:, 0:256])))(),
                 reads=("vf32",), writes=(("o_mvp", seq, j),), dma="vf32")
            chk(1.5)
            vaug_write(vM[:, 0, j, :], ps[:, b, 256:512], 4, (("ps", b),), ("vM", 0))
            chk(1.6)
            transposes_to([qn[:, 0:128], qn[:, 128:256]], [(0, 2, mkT[:, 0, :, j * 128:(j + 1) * 128], (("mkT", 0),))], (("qn", 0),), [6 + j])
        ws_done(4)

    def load_rows_T(src_ap, ncols, dup, dest_ap, wkey, bank):
        P.op("pool", lambda e: e.dma_start(out=qn[:, 0:ncols], in_=src_ap), writes=(("qn", 0),), dma="qn_in")
        kad_ = KAD[0]
        if dup:
            assert ncols == 128
            P.op("act", lambda e: e.activation(out=kad_.rearrange("p (k two d) -> p k two d", two=2, d=64),
                                               in_=qn[:, 0:128].rearrange("p (k d) -> p k d", d=64).unsqueeze(2).to_broadcast([128, 2, 2, 64]),
                                               func=AF.Copy), reads=(("qn", 0),), writes=(("kadup", 0),))
            srcs = [kad_[:, 0:128], kad_[:, 128:256]]
        else:
            srcs = [qn[:, t * 128:(t + 1) * 128] for t in range(ncols // 128)]
        transposes_to(srcs, [(0, len(srcs), dest_ap, (wkey,))], (("qn", 0), ("kadup", 0)), [bank])

    def load_rows_V(src_ap, nh_src, dst_tile_ap, wkey):
        w = nh_src * 64
        P.op("pool", lambda e: e.dma_start(out=vf32[:, 0:w], in_=src_ap), writes=("vf32",), dma="vf32_in")
        vaug_write(dst_tile_ap, vf32[:, 0:w], nh_src, ("vf32",), wkey)

    rpar_ctr = [0]
    keep_alive = []

    def x_load(seq, ti, subs=(0, 1, 2, 3), alt=None):
        for s in subs:
            r0 = ti * 512 + s * 128
            xa_, xk_ = XB(s, alt)
            P.op("pool", (lambda xa_=xa_, r0=r0: (lambda e: e.dma_start(out=xa_, in_=xp[seq, r0:r0 + 128, :])))(),
                 writes=(xk_,), dma=xk_)

    def main_program():
      chk(0)
      ws_record_loads()
      chk(1)
      tiles = [(seq, ti) for seq in range(2) for ti in range(NT)]
      prefetched = False
      for gi, (seq, ti) in enumerate(tiles):
            if ti == 0:
                mem_phase(seq)
                chk(2)
            first = (gi == 0)
            XCUR["alt"] = (gi >= 2 and gi % 2 == 0)
            if not prefetched:
                x_load(seq, ti)
                pp = NormPipe(0)
                for s in range(4):
                    pp.feed(s)
                pp.flush()
            chk(3)
            pp = NormPipe(1)
            ffn("ff1", 4, post_sub=pp.feed)
            chk(4)
            info = dict(kind="prompt", seq=seq, ti=ti, rpar=rpar_ctr[0] % 2, dbl=not first, npipe=pp)
            rpar_ctr[0] += 1
            mixer(4, info)
            chk(5)
            pp = NormPipe(2)
            post_attention(4, post_sub=pp.feed)
            pp.flush()
            chk(6)
            fin = FinalPipe((lambda seq=seq, ti=ti: (lambda s: yp[seq, ti * 512 + s * 128: ti * 512 + (s + 1) * 128, :]))())
            nxt = tiles[gi + 1] if gi + 1 < len(tiles) else None
            do_pf = PREFETCH_X and nxt is not None and nxt[1] != 0 and gi + 1 >= 2
            if do_pf:
                nalt = ((gi + 1) % 2 == 0)
                assert nalt != XCUR["alt"]
                x_load(nxt[0], nxt[1], subs=(1, 2, 3), alt=nalt)
                npipe2 = NormPipe(0, alt=nalt, so=72, bank=6)
                order = {0: 1, 1: 2, 2: 3}

                def post(s, fin=fin, npipe2=npipe2, order=order):
                    fin.feed(s)
                    if s in order:
                        npipe2.feed(order[s])
                ffn("ff2", 4, post_sub=post)
                fin.flush()
                x_load(nxt[0], nxt[1], subs=(0,), alt=nalt)
                npipe2.feed(0)
                npipe2.flush()
                prefetched = True
            else:
                ffn("ff2", 4, post_sub=fin.feed)
                fin.flush()
                prefetched = False
            chk(7)
      chk(8)
      if do_sample:
        sample_program()
      assert ws["pos"] == len(stream), (ws["pos"], len(stream))

    def sample_program():
        P.op("pool", lambda e: e.dma_start(out=x[:, 0, :], in_=xs_in), writes=(("x", 0),), dma=("x", 0))
        norm_to_hT(x[:, 0, :], ("x", 0), 0, 0, 0)
        ffn("ff1", 1)
        norm_to_hT(x[:, 0, :], ("x", 0), 1, 0, 0)
        info = dict(kind="sample", rpar=rpar_ctr[0] % 2, dbl=False)
        mixer(1, info)
        for sseq in range(2):
            for j in range(2):
                load_rows_T(cmk[sseq, j * 128:(j + 1) * 128, :], 256, False, mkT[:, sseq, :, j * 128:(j + 1) * 128], ("mkT", sseq), 6 + j)
                load_rows_V(cmv[sseq, j * 128:(j + 1) * 128, :], 4, vM[:, sseq, j, :], ("vM", sseq))
            load_rows_T(cak[sseq], 128, True, kaT[:, :, 7 * 128:8 * 128], ("kaT", 7), 6)
            load_rows_V(cav[sseq], 2, vA[:, 7, :], ("vA", 7))
            for j in range(4):
                load_rows_T(cbk[sseq, j * 128:(j + 1) * 128, :], 256, False, kbT[:, :, (4 + j) * 128:(5 + j) * 128], ("kbT", 4 + j), 6 + j % 2)
                load_rows_V(cbv[sseq, j * 128:(j + 1) * 128, :], 4, vB[:, 4 + j, :], ("vB", 4 + j))
            own = 16 + sseq
            klA = keylist([(14, 2), (15, 1), (own, 0)])
            klB = keylist([(kc, 16 - kc) for kc in range(8, 16)] + [(own, 0)])
            stg_ = attend_chunk(sseq * 64, klA, klB)
            stg_.sa()
            stg_.sb()
            stg_.pva()
            stg_.pvb()
            attend_mem(sseq * 64, 64, sseq)
        post_attention(1)
        norm_to_hT(x[:, 0, :], ("x", 0), 2, 0, 0)
        ffn("ff2", 1)
        final_A(0)
        final_B(0, lambda s: ys)

    try:
        main_program()
    except _Stop:
        pass
    okeys = tuple(k for k in P.last_writer.keys() if isinstance(k, tuple) and isinstance(k[0], str) and k[0].startswith("o_")) + \
        tuple(k for k in P.last_writer.keys() if isinstance(k, str) and k.startswith("o_"))
    fin = P.op("pool", None, reads=okeys)
    for en in P.ENGS:
        cand = [o for o in P.ops[en] if o is not fin and o.fn is not None]
        if cand:
            lo = cand[-1]
            lo.signals = True
            if lo not in fin.deps:
                fin.deps.append(lo)
    P.emit(nc)
    st.close()
    return nc


def _consts(SEQ):
    half = 32
    inv = (10000.0 ** (-np.arange(half, dtype=np.float32) / half)).astype(np.float32)

    def tab(pos):
        ang = pos.astype(np.float32)[:, None] * inv[None, :]
        return np.concatenate([np.cos(ang), np.sin(ang)], axis=1).astype(np.float32)
    ropeP = tab(np.arange(SEQ))
    ropeS = tab(1024 + (np.arange(128) % 64))
    ident = np.eye(128, dtype=np.float32)
    zmat = np.zeros((128, 192), np.float32)
    zmat[0:64, 64:128] = np.eye(64, dtype=np.float32)
    zmat[64:128, 64:128] = np.eye(64, dtype=np.float32)
    return ropeP, ropeS, ident, zmat


def _prep_shared(inp, SEQ):
    L = 0
    f = lambda a: np.ascontiguousarray(a, dtype=np.float32)
    w_in = inp["w_in"][L]
    perm = np.concatenate([np.arange(0, 512), np.arange(512, 640), np.arange(768, 1024), np.arange(1024, 1280),
                           np.arange(1536, 1792), np.arange(640, 768), np.arange(1280, 1536)])
    sh = {}
    sh["w_ff1_gate"] = f(inp["w_ff1_gate"][L])
    sh["w_ff1_up"] = f(inp["w_ff1_up"][L])
    sh["w_ff1_down"] = f(inp["w_ff1_down"][L])
    sh["w_ff2_gate"] = f(inp["w_ff2_gate"][L])
    sh["w_ff2_up"] = f(inp["w_ff2_up"][L])
    sh["w_ff2_down"] = f(inp["w_ff2_down"][L])
    sh["w_in"] = f(w_in[:, perm])
    sh["w_mem"] = f(inp["w_mem_kv"][L])
    sh["w_gate"] = f(inp["w_gate"][L])
    sh["w_bra"] = f(inp["w_br_a"][L]); sh["w_brb"] = f(inp["w_br_b"][L]); sh["w_brc"] = f(inp["w_br_c"][L])
    sh["w_out"] = f(inp["w_out"][L])
    gs = np.stack([inp["g_ff1"][L], inp["g_mix"][L], inp["g_ff2"][L], inp["g_mem"][L]], 0)
    sh["gT"] = f(gs.reshape(4, 8, 128).transpose(2, 0, 1))
    sh["gfin"] = f(inp["g_final"][L].reshape(1, D))
    gqk = np.concatenate([np.tile(inp["g_qa"][L], 8), np.tile(inp["g_ka"][L], 2), np.tile(inp["g_qb"][L], 4),
                          np.tile(inp["g_kb"][L], 4), np.tile(inp["g_qc"][L], 4)])
    sh["gqk"] = f(gqk.reshape(1, 1408))
    sh["gkc"] = f(np.tile(inp["g_kc"][L], 4).reshape(1, 256))
    sh["bgT"] = f(inp["b_gate"][L].reshape(24, 128).T)
    sh["sinks"] = f(inp["sinks_a"][L].reshape(1, 8))
    rb = inp["rel_bias_b"][L]
    kk = np.arange(64)[:, None, None]
    jj = np.arange(3)[None, :, None]
    qq = np.arange(64)[None, None, :]
    idx = np.clip(64 * jj + qq - kk, -128, 128) + 128
    bt = rb[:, idx].transpose(1, 0, 2, 3)
    sh["biasT"] = f(np.concatenate([bt, bt], axis=0))
    sh["biasc"] = f(np.tile(rb[:, 256].reshape(1, 4), (128, 1)))
    ropeP, ropeS, ident, zmat = _consts(SEQ)
    sh["ropeP"] = ropeP; sh["ropeS"] = ropeS; sh["ident"] = ident; sh["zmat"] = zmat
    return sh


def _core_inputs(inp, sh, c, SEQ):
    f = lambda a: np.ascontiguousarray(a, dtype=np.float32)
    b0 = 2 * c
    m = dict(sh)
    m["xp"] = f(inp["x_prompt"][b0:b0 + 2, :SEQ])
    m["xs"] = f(inp["x_sample"][b0:b0 + 2].reshape(128, D))
    m["cak"] = f(inp["cache_a_k"][0, b0:b0 + 2].reshape(2, 128, 128))
    m["cav"] = f(inp["cache_a_v"][0, b0:b0 + 2].reshape(2, 128, 128))
    m["cbk"] = f(inp["cache_b_k"][0, b0:b0 + 2].reshape(2, 512, 256))
    m["cbv"] = f(inp["cache_b_v"][0, b0:b0 + 2].reshape(2, 512, 256))
    m["cmk"] = f(inp["cache_mem_k"][0, b0:b0 + 2].reshape(2, 256, 256))
    m["cmv"] = f(inp["cache_mem_v"][0, b0:b0 + 2].reshape(2, 256, 256))
    m["memp"] = f(inp["mem_prompt"][b0:b0 + 2])
    return m


_NC_CACHE = {}


def run_cores(inp, SEQ, ncores, trace=False):
    if SEQ not in _NC_CACHE:
        _NC_CACHE[SEQ] = build_nc(SEQ)
    nc = _NC_CACHE[SEQ]
    sh = _prep_shared(inp, SEQ)
    in_maps = [_core_inputs(inp, sh, c, SEQ) for c in range(ncores)]
    res = run_bass_kernel_spmd(nc, in_maps, core_ids=list(range(ncores)), trace=trace)
    return res


def assemble(results, SEQ, ncores):
    B = 2 * ncores
    R = results
    cat = lambda name, shp: np.concatenate([np.asarray(r[name]).reshape(shp) for r in R], axis=0)
    y_p = cat("yp", (2, SEQ, D))
    y_s = cat("ys", (2, 64, D))
    akp = cat("akp", (2, 128, 2, 64))[None]
    avp = cat("avp", (2, 128, 2, 64))[None]
    bkp = cat("bkp", (2, 512, 4, 64))[None]
    bvp = cat("bvp", (2, 512, 4, 64))[None]
    mkp = cat("mkp", (2, 256, 4, 64))[None]
    mvp = cat("mvp", (2, 256, 4, 64))[None]
    aks = cat("aks", (2, 64, 2, 64))[None]
    avs = cat("avs", (2, 64, 2, 64))[None]
    bks = cat("bks", (2, 64, 4, 64))[None]
    bvs = cat("bvs", (2, 64, 4, 64))[None]
    return tuple(np.ascontiguousarray(a, dtype=np.float32) for a in
                 (y_p, y_s, akp, avp, bkp, bvp, mkp, mvp, aks, avs, bks, bvs))


def kernel(**inputs):
    inp = {k: np.asarray(v) for k, v in inputs.items()}
    SEQ = inp["x_prompt"].shape[1]
    res = run_cores(inp, SEQ, NCORES)
    return assemble(res.results, SEQ, NCORES)
```
